# Optimizing a Trainium2 kernel written in Bass

```python
import jax, jax.numpy as jnp
from jax import lax
import numpy as np

D_MODEL = 1024
BATCH = 4
SEQ = 4096
DEPTH = 1

CHUNK = 64
N_META = 16
Q_BLOCK = 128
D_CONV = 512
CONV_WIDTH = 31
N_HEADS = 8
QK_NOPE = 64
QK_ROPE = 32
V_HEAD = 64
D_ATTN = N_HEADS * V_HEAD
Q_LORA = 384
KV_LORA = 256
ROPE_THETA = 10000.0
D_MIX = D_CONV + D_ATTN
D_IN = 2 * D_CONV + Q_LORA + KV_LORA + QK_ROPE
D_FF = 2816
FFN_CONV_WIDTH = 3
EPS = 1e-6
NEG = -1e30

kernel_name = 'hymba_conformer_mla_convffn_block'


def rms_norm(x, g):
    xf = x.astype(jnp.float32)
    y = xf * lax.rsqrt(jnp.mean(xf * xf, axis=-1, keepdims=True) + EPS)
    return (y * g.astype(jnp.float32)).astype(x.dtype)


def layer_norm(x, g, b):
    xf = x.astype(jnp.float32)
    mu = jnp.mean(xf, axis=-1, keepdims=True)
    var = jnp.mean(jnp.square(xf - mu), axis=-1, keepdims=True)
    y = (xf - mu) * lax.rsqrt(var + EPS)
    return (y * g.astype(jnp.float32) + b.astype(jnp.float32)).astype(x.dtype)


def causal_depthwise_conv(x, w, b):
    k = w.shape[0]
    y = lax.conv_general_dilated(
        x, w[:, None, :].astype(x.dtype), window_strides=(1,), padding=[(k - 1, 0)],
        dimension_numbers=('NWC', 'WIO', 'NWC'), feature_group_count=x.shape[-1])
    return y + b.astype(x.dtype)


def rope(x, cos, sin):
    half = x.shape[-1] // 2
    x1, x2 = x[..., :half], x[..., half:]
    return jnp.concatenate([x1 * cos - x2 * sin, x2 * cos + x1 * sin], axis=-1)


def block_causal_attention(q, k, v, chunk_id):
    b, l, h, dqk = q.shape
    dv = v.shape[-1]
    nblk = l // Q_BLOCK
    qb = q.reshape(b, nblk, Q_BLOCK, h, dqk).transpose(1, 0, 2, 3, 4)
    cb = chunk_id.reshape(nblk, Q_BLOCK)
    scale = dqk ** -0.5

    def one_block(args):
        qi, ci = args
        s = jnp.einsum('bqhd,bkhd->bhqk', qi, k, preferred_element_type=jnp.float32) * scale
        visible = ci[:, None] >= chunk_id[None, :]
        s = jnp.where(visible[None, None], s, NEG)
        p = jax.nn.softmax(s, axis=-1)
        return jnp.einsum('bhqk,bkhd->bqhd', p.astype(v.dtype), v)

    o = lax.map(one_block, (qb, cb))
    return o.transpose(1, 0, 2, 3, 4).reshape(b, l, h * dv)


def hybrid_layer(h, cos, sin, chunk_id, mix_norm_g, w_in, q_norm_g, w_uq, kv_norm_g, w_ukv,
                 conv_w, conv_b, conv_ln_g, conv_ln_b, conv_out_g, attn_out_g, w_out,
                 ffn_norm_g, w_ffn_up, ffn_conv_w, ffn_conv_b, w_ffn_down):
    b, l, _ = h.shape
    n = rms_norm(h, mix_norm_g)
    z = n @ w_in.astype(h.dtype)
    a, gate, c_q, c_kv, k_r = jnp.split(
        z, [D_CONV, 2 * D_CONV, 2 * D_CONV + Q_LORA, 2 * D_CONV + Q_LORA + KV_LORA], axis=-1)

    u = a * jax.nn.sigmoid(gate)
    u = causal_depthwise_conv(u, conv_w, conv_b)
    u = jax.nn.silu(layer_norm(u, conv_ln_g, conv_ln_b))

    q = (rms_norm(c_q, q_norm_g) @ w_uq.astype(h.dtype)).reshape(b, l, N_HEADS, QK_NOPE + QK_ROPE)
    kv = (rms_norm(c_kv, kv_norm_g) @ w_ukv.astype(h.dtype)).reshape(b, l, N_HEADS, QK_NOPE + V_HEAD)
    q_nope, q_rot = q[..., :QK_NOPE], q[..., QK_NOPE:]
    k_nope, v = kv[..., :QK_NOPE], kv[..., QK_NOPE:]
    q_rot = rope(q_rot, cos[:, None, :], sin[:, None, :])
    k_rot = rope(k_r, cos, sin)
    qf = jnp.concatenate([q_nope, q_rot], axis=-1)
    kf = jnp.concatenate(
        [k_nope, jnp.broadcast_to(k_rot[:, :, None, :], (b, l, N_HEADS, QK_ROPE))], axis=-1)
    o = block_causal_attention(qf, kf, v, chunk_id)

    mix = jnp.concatenate([rms_norm(u, conv_out_g), rms_norm(o, attn_out_g)], axis=-1)
    h = h + mix @ w_out.astype(h.dtype)

    n2 = rms_norm(h, ffn_norm_g)
    up = causal_depthwise_conv(n2 @ w_ffn_up.astype(h.dtype), ffn_conv_w, ffn_conv_b)
    g, val = up[..., :D_FF], up[..., D_FF:]
    return h + (jax.nn.silu(g) * val) @ w_ffn_down.astype(h.dtype)


def setup_inputs(seed: int = 0) -> dict:
    key = jax.random.key(seed)
    ks = jax.random.split(key, 24)
    f32 = jnp.float32

    def nrm(k, shape, scale):
        return jax.random.normal(k, shape, f32) * scale

    def gain(k, shape):
        return 1.0 + 0.02 * jax.random.normal(k, shape, f32)

    L = DEPTH
    return {
        'x': jax.random.normal(ks[0], (BATCH, SEQ, D_MODEL), f32),
        'meta_tokens': nrm(ks[1], (N_META, D_MODEL), 1.0),
        'mix_norm_g': gain(ks[2], (L, D_MODEL)),
        'w_in': nrm(ks[3], (L, D_MODEL, D_IN), D_MODEL ** -0.5),
        'q_norm_g': gain(ks[4], (L, Q_LORA)),
        'w_uq': nrm(ks[5], (L, Q_LORA, N_HEADS * (QK_NOPE + QK_ROPE)), Q_LORA ** -0.5),
        'kv_norm_g': gain(ks[6], (L, KV_LORA)),
        'w_ukv': nrm(ks[7], (L, KV_LORA, N_HEADS * (QK_NOPE + V_HEAD)), KV_LORA ** -0.5),
        'conv_w': nrm(ks[8], (L, CONV_WIDTH, D_CONV), CONV_WIDTH ** -0.5),
        'conv_b': nrm(ks[9], (L, D_CONV), 0.02),
        'conv_ln_g': gain(ks[10], (L, D_CONV)),
        'conv_ln_b': nrm(ks[11], (L, D_CONV), 0.02),
        'conv_out_g': gain(ks[12], (L, D_CONV)),
        'attn_out_g': gain(ks[13], (L, D_ATTN)),
        'w_out': nrm(ks[14], (L, D_MIX, D_MODEL), D_MIX ** -0.5),
        'ffn_norm_g': gain(ks[15], (L, D_MODEL)),
        'w_ffn_up': nrm(ks[16], (L, D_MODEL, 2 * D_FF), D_MODEL ** -0.5),
        'ffn_conv_w': nrm(ks[17], (L, FFN_CONV_WIDTH, 2 * D_FF), FFN_CONV_WIDTH ** -0.5),
        'ffn_conv_b': nrm(ks[18], (L, 2 * D_FF), 0.02),
        'w_ffn_down': nrm(ks[19], (L, D_FF, D_MODEL), D_FF ** -0.5),
        'final_norm_g': gain(ks[20], (D_MODEL,)),
    }


def reference(x, meta_tokens, mix_norm_g, w_in, q_norm_g, w_uq, kv_norm_g, w_ukv,
              conv_w, conv_b, conv_ln_g, conv_ln_b, conv_out_g, attn_out_g, w_out,
              ffn_norm_g, w_ffn_up, ffn_conv_w, ffn_conv_b, w_ffn_down, final_norm_g):
    b, s, d = x.shape
    l_real = N_META + s
    l_pad = ((l_real + Q_BLOCK - 1) // Q_BLOCK) * Q_BLOCK
    meta = jnp.broadcast_to(meta_tokens[None].astype(x.dtype), (b, N_META, d))
    pad = jnp.zeros((b, l_pad - l_real, d), x.dtype)
    h = jnp.concatenate([meta, x, pad], axis=1)

    pos = jnp.arange(l_pad, dtype=jnp.int32)
    chunk_id = jnp.where(pos < N_META, 0, 1 + (pos - N_META) // CHUNK).astype(jnp.int32)
    inv_freq = 1.0 / (ROPE_THETA ** (jnp.arange(0, QK_ROPE, 2, dtype=jnp.float32) / QK_ROPE))
    ang = pos.astype(jnp.float32)[:, None] * inv_freq[None, :]
    cos = jnp.cos(ang).astype(x.dtype)
    sin = jnp.sin(ang).astype(x.dtype)

    for i in range(DEPTH):
        h = hybrid_layer(h, cos, sin, chunk_id, mix_norm_g[i], w_in[i], q_norm_g[i], w_uq[i],
                         kv_norm_g[i], w_ukv[i], conv_w[i], conv_b[i], conv_ln_g[i], conv_ln_b[i],
                         conv_out_g[i], attn_out_g[i], w_out[i], ffn_norm_g[i], w_ffn_up[i],
                         ffn_conv_w[i], ffn_conv_b[i], w_ffn_down[i])

    h = rms_norm(h, final_norm_g)
    return h[:, N_META:N_META + s]
```

```python
import numpy as np
import ml_dtypes
from contextlib import ExitStack
import concourse.bass as bass
import concourse.mybir as mybir
from concourse.bass_utils import run_bass_kernel_spmd

F32 = mybir.dt.float32
BF = mybir.dt.bfloat16
AF = mybir.ActivationFunctionType
ALU = mybir.AluOpType

D = 1024
NWIN = 2080
HALO = 32
NOWN = 2048
NPRE = 2048
NMETA = 16
NK = NPRE + NMETA + NOWN
KOWN = NPRE + NMETA
EPS = 1e-6
SCALE = 96 ** -0.5
NEGM = -30000.0
UPAD = 30
V_GOUT = 0
V_CW = 8
V_CB = V_CW + 124
V_LG = V_CB + 4
V_LB = V_LG + 4
V_FW = V_LB + 4
V_FB = V_FW + 132
NV = V_FB + 44

FFN_MACROS = [[(32, 510), (542, 173)], [(715, 510), (1225, 173)], [(1398, 510), (1908, 172)]]


class Buf:
    __slots__ = ("w", "r", "x")

    def __init__(self, excl=False):
        self.w = None
        self.r = {}
        self.x = excl


class Prog:
    ENG = ("pe", "act", "dve", "pool", "sp")
    NSLOT = 8

    def __init__(self):
        self.q = {e: [] for e in self.ENG}
        self.cnt = {e: 0 for e in ("pe", "act", "dve", "pool")}
        self.seen = {e: {} for e in self.ENG}
        self.dma_n = {e: 0 for e in self.ENG}
        self.semkeys = set(["pe", "act", "dve", "pool"])

    def _waits(self, eng, reads, writes, extra=()):
        deps = {}

        def add(k, v):
            if v > deps.get(k, 0):
                deps[k] = v
        for b in reads:
            if b.w:
                add(*b.w)
            if b.x:
                for k, v in b.r.items():
                    if k != eng:
                        add(k, v)
        for b in writes:
            if b.w:
                add(*b.w)
            for k, v in b.r.items():
                add(k, v)
        for k, v in extra:
            add(k, v)
        waits = []
        for k, v in deps.items():
            if k == eng and eng == "pe":
                continue
            if self.seen[eng].get(k, 0) < v:
                waits.append((k, v))
                self.seen[eng][k] = v
        return waits

    def op(self, eng, fn, reads=(), writes=()):
        waits = self._waits(eng, reads, writes)
        self.cnt[eng] += 1
        c = self.cnt[eng]
        self.q[eng].append((waits, fn, (eng, 1)))
        for b in reads:
            if b.r.get(eng, 0) < c:
                b.r[eng] = c
        for b in writes:
            b.w = (eng, c)
            b.r = {}

    def dma(self, queue, fn, reads=(), writes=()):
        i = self.dma_n[queue]
        self.dma_n[queue] += 1
        key = ("dma", queue, i % self.NSLOT)
        self.semkeys.add(key)
        prev = 16 * (i // self.NSLOT)
        target = prev + 16
        waits = self._waits(queue, reads, writes, extra=((key, prev),) if prev else ())
        self.q[queue].append((waits, fn, (key, 16)))
        for b in reads:
            b.r[key] = target
        for b in writes:
            b.w = (key, target)
            b.r = {}

    def barrier(self):
        allk = {k: v for k, v in self.cnt.items()}
        for q in self.ENG:
            n = self.dma_n[q]
            for sl in range(min(n, self.NSLOT)):
                cntsl = (n - sl + self.NSLOT - 1) // self.NSLOT
                allk[("dma", q, sl)] = 16 * cntsl
        for e in self.ENG:
            waits = []
            for k, v in allk.items():
                if v and self.seen[e].get(k, 0) < v and not (k == e and e == "pe"):
                    waits.append((k, v))
                    self.seen[e][k] = v
            if waits:
                self.q[e].append((waits, None, None))


class Arena:
    def __init__(self, nc, lo=16512, hi=229344):
        self.nc = nc
        self.lo = lo
        self.hi = hi
        self.n = 0

    def _size(self, shape, dt):
        n = 1
        for x in shape[1:]:
            n *= x
        b = n * (2 if dt == BF else 4)
        return (b + 31) // 32 * 32

    def up(self, name, shape, dt=F32):
        sz = self._size(shape, dt)
        off = self.lo
        self.lo += sz
        assert self.lo <= self.hi, ("SBUF overflow", name, self.lo, self.hi)
        self.n += 1
        return self.nc.alloc_sbuf_tensor_at("%s_%d" % (name, self.n), list(shape), dt, offset=off)

    def down(self, name, shape, dt=F32):
        sz = self._size(shape, dt)
        self.hi -= sz
        assert self.lo <= self.hi, ("SBUF overflow", name, self.lo, self.hi)
        self.n += 1
        return self.nc.alloc_sbuf_tensor_at("%s_%d" % (name, self.n), list(shape), dt, offset=self.hi)


class Pipe:
    def __init__(self):
        self.units = []

    def unit(self, stages, lead=0):
        self.units.append([None] * lead + list(stages))

    def emit(self):
        U = len(self.units)
        if not U:
            return
        S = max(len(u) for u in self.units)
        for t in range(U + S):
            for u in range(max(0, t - S + 1), min(U - 1, t) + 1):
                st = self.units[u]
                k = t - u
                if k < len(st) and st[k] is not None:
                    st[k]()
        self.units = []


class Rot:
    def __init__(self, items):
        self.items = items
        self.i = 0

    def next(self):
        x = self.items[self.i % len(self.items)]
        self.i += 1
        return x


def blocks_of(n, bs=128):
    out = []
    r = 0
    while r < n:
        out.append((r, min(bs, n - r)))
        r += bs
    return out


def build(debug=None, stop=99):
    nc = bass.Bass("TRN2", target_bir_lowering=False)
    P = Prog()
    es = ExitStack()

    def dram(name, shape, dt=F32, out=False):
        return nc.dram_tensor(name, list(shape), dt, kind="ExternalOutput" if out else "ExternalInput")

    xw_d = dram("xw", [NWIN, D])
    xp_d = dram("xp", [NPRE, D])
    meta_d = dram("meta", [NMETA, D])
    csk_d = dram("csk", [128, 33, 32])
    csq_d = dram("csq", [128, 17, 32])
    csq4_d = dram("csq4", [128, 17, 128])
    maskk_d = dram("maskk", [9, NK], BF)
    maskq_d = dram("maskq", [9, NWIN], BF)
    ident_d = dram("ident", [128, 128], BF)
    vecs_d = dram("vecs", [128, NV])
    gin_d = dram("gin", [128, D])
    gq_d = dram("gq", [128, 384])
    gkv_d = dram("gkv", [128, 256])
    gffn_d = dram("gffn", [128, D])
    gfin_d = dram("gfin", [128, D])
    win_d = dram("win", [128, 8, 1696])
    wuq_d = dram("wuq", [128, 3, 768])
    wk_d = dram("wk", [128, 2, 512])
    wv_d = dram("wv", [128, 2, 512])
    wout_d = dram("wout", [128, 8, D])
    wup_d = dram("wup", [22, 128, 8, 256])
    wdn_d = dram("wdn", [128, 22, D])
    out_d = dram("out", [NOWN, D], out=True)
    dbg_d = {}
    if debug:
        for nm, shp in debug.items():
            dbg_d[nm] = dram("dbg_" + nm, shp, out=True)

    A = Arena(nc)

    def sb(name, shape, dt=F32):
        return A.up(name, shape, dt)

    ident = sb("ident", [128, 128], BF)
    onesm = sb("onesm", [128, 128], BF)
    onecol = sb("onecol", [128, 1], BF)
    onesf = sb("onesf", [128, 64], F32)
    vecs = sb("vecs", [128, NV])
    swT = sb("swT", [128, 4, NWIN], BF)
    small = sb("small", [128, 64])
    ps = es.enter_context(nc.psum_tensor("ps", [128, 8, 512], F32))
    psb = ps.bitcast(BF)
    B_ps = [Buf(True) for _ in range(8)]
    B_const = Buf()
    B_swT = [Buf() for _ in range(5)]
    B_oT = [Buf() for _ in range(5)]
    smalls = Rot([(small[:, i:i + 1], Buf()) for i in range(64)])

    def wtile(col):
        return 0 if col < HALO else 1 + (col - HALO) // 512

    def wtiles(c0, n):
        return sorted(set([wtile(c0), wtile(c0 + n - 1)]))

    def act(out, in_, func, bias=0.0, scale=1.0, accum_out=None):
        if accum_out is not None:
            return lambda e: e.activation(out, in_, func, bias=bias, scale=scale, accum_out=accum_out)
        return lambda e: e.activation(out, in_, func, bias=bias, scale=scale)

    def tt(out, a, b, op):
        return lambda e: e.tensor_tensor(out, a, b, op)

    def ts(out, a, s1, s2, op0, op1=None):
        if op1 is None:
            return lambda e: e.tensor_scalar(out, a, s1, None, op0)
        return lambda e: e.tensor_scalar(out, a, s1, s2, op0, op1)

    def stt(out, in0, scalar, in1, op0, op1):
        return lambda e: e.scalar_tensor_tensor(out, in0, scalar, in1, op0, op1)

    def cp(out, in_):
        return lambda e: e.tensor_copy(out, in_)

    def dmaf(out, in_):
        return lambda e: e.dma_start(out, in_)

    def mm_group(out, pairs):
        def fn(e):
            ins = None
            n = len(pairs)
            for i, (l, r) in enumerate(pairs):
                ins = e.matmul(out, l, r, start=(i == 0), stop=(i == n - 1))
            return ins
        return fn

    def rstd_from(ssq_ap, ssq_buf, nb, inv_n):
        t_ap, t_b = smalls.next()
        r_ap, r_b = smalls.next()
        P.op("act", act(t_ap[:nb], ssq_ap[:nb], AF.Sqrt, bias=EPS, scale=inv_n), [ssq_buf], [t_b])
        P.op("dve", lambda e: e.reciprocal(r_ap[:nb], t_ap[:nb]), [t_b], [r_b])
        return r_ap, r_b

    P.dma("act", dmaf(ident[:], ident_d.ap()), [], [B_const])
    P.dma("act", dmaf(vecs[:], vecs_d.ap()), [], [B_const])
    B_c2 = Buf()
    P.op("pool", lambda e: e.memset(onesm[:], 1.0 / 512.0), [], [B_c2])
    P.op("pool", lambda e: e.memset(onecol[:], 1.0), [], [B_c2])
    P.op("pool", lambda e: e.memset(onesf[:], 1.0), [], [B_c2])
    CONST = [B_const, B_c2]

    mark_persist = A.lo
    ckvnT = sb("ckvnT", [128, 2, NK], BF)
    krotT = sb("krotT", [32, NK], BF)
    cqnT = sb("cqnT", [128, 3, NWIN], BF)
    wuq = sb("wuq", [128, 3, 768], BF)
    wk = sb("wk", [128, 2, 512], BF)
    wv = sb("wv", [128, 2, 512], BF)
    mark_lat = A.lo
    B_lat = Buf()
    B_w3 = Buf()

    def load_cast(dst, src_dram_ap, wbuf):
        P.dma("pool", lambda e: e.dma_start(dst, src_dram_ap, max_dma_last_dim=8192), [], [wbuf])


    if True:
        sb1 = sb
        diag = sb1("diag", [128, 4, 31, 128], BF)
        uT = sb1("uT", [128, 4, UPAD + NWIN], BF)
        mark1 = A.lo
        win = sb1("win", [128, 8, 1696], BF)
        gin = sb1("gin", [128, D])
        gq = sb1("gq", [128, 384])
        gkv = sb1("gkv", [128, 256])
        csk = sb1("csk", [128, 33, 32])
        csq = sb1("csq", [128, 17, 32])
        xts = [sb1("xt%d" % i, [128, D]) for i in range(3)]
        junk = sb1("junk", [128, D], BF)
        nbs = [sb1("nb%d" % i, [128, D], BF) for i in range(2)]
        nTs = [sb1("nT%d" % i, [128, 8, 512], BF) for i in range(3)]
        lat_sb = [sb1("latsb%d" % i, [128, 384], BF) for i in range(4)]
        krot_sb = [sb1("krotsb%d" % i, [128, 32], BF) for i in range(2)]
        rtmp = sb1("rtmp", [128, 4, 16])
        sig_sb = [sb1("sig%d" % i, [128, 512]) for i in range(2)]

        B_win = Buf()
        def win_load(ca, cb_):
            for k0 in range(0, 8, 4):
                load_cast(win[:, k0:k0 + 4, ca:cb_], win_d.ap()[:, k0:k0 + 4, ca:cb_], B_win)
        win_load(1408, 1696)

        def late_loads(g):
            def f():
                if g == 0:
                    win_load(1024, 1408)
                    win_load(0, 512)
                    win_load(512, 1024)
                if g == 2:
                    for kc in range(3):
                        load_cast(wuq[:, kc, :], wuq_d.ap()[:, kc, :], B_w3)
                    for kc in range(2):
                        load_cast(wk[:, kc, :], wk_d.ap()[:, kc, :], B_w3)
                        load_cast(wv[:, kc, :], wv_d.ap()[:, kc, :], B_w3)
            return f
        B_g = Buf()
        P.dma("act", dmaf(gin[:], gin_d.ap()), [], [B_g])
        P.dma("act", dmaf(gq[:], gq_d.ap()), [], [B_g])
        P.dma("act", dmaf(gkv[:], gkv_d.ap()), [], [B_g])
        P.dma("act", dmaf(csk[:], csk_d.ap()), [], [B_g])
        B_uT = [Buf() for _ in range(5)]
        B_upad = Buf()
        P.op("dve", lambda e: e.memset(uT[:, :, 0:UPAD], 0.0), [], [B_upad])
        B_diag = Buf()

        def diag_unit(i):
            def f():
                for k in range(31):
                    c = V_CW + i * 31 + k
                    P.op("dve", ts(diag[:, i, k, :], ident[:], vecs[:, c:c + 1], None, ALU.mult), CONST, [B_diag])
            return f

        xt_r = Rot([(xts[i], Buf()) for i in range(3)])
        nb_r = Rot([(nbs[i], Buf()) for i in range(2)])
        nT_r = Rot([(nTs[i], Buf()) for i in range(3)])
        lat_r = Rot([(lat_sb[i], Buf()) for i in range(4)])
        krot_r = Rot([(krot_sb[i], Buf()) for i in range(2)])
        sig_r = Rot([(sig_sb[i], Buf()) for i in range(2)])
        B_junk = Buf()
        B_rtmp = Buf()
        tp_r = Rot([0, 1])
        tpb_r = Rot([2, 3])
        lt_r = Rot([4, 5])
        ag_r = Rot([(6, 7)])
        pipe1 = Pipe()

        def norm_block(src_rows_ap, nb, gain, nT, nTbuf, r0):
            xt, xb = xt_r.next()
            P.dma("sp", dmaf(xt[:nb], src_rows_ap), [], [xb])
            s_ap, s_b = smalls.next()
            P.op("act", act(junk[:nb], xt[:nb], AF.Square, accum_out=s_ap[:nb]), [xb], [B_junk, s_b])
            r_ap, r_b = rstd_from(s_ap, s_b, nb, 1.0 / D)
            nbt, nbb = nb_r.next()
            P.op("dve", stt(nbt[:nb], xt[:nb], r_ap[:nb], gain[:nb], ALU.mult, ALU.mult), [xb, r_b, B_g], [nbb])
            bank = tp_r.next()

            def trs(e):
                ins = None
                for kc in range(8):
                    ins = e.transpose(psb[:, bank, kc * 128:kc * 128 + nb], nbt[:nb, kc * 128:(kc + 1) * 128], ident[:nb, :nb])
                return ins
            P.op("pe", trs, [nbb] + CONST, [B_ps[bank]])
            src = psb[:, bank, :].rearrange("p (k t) -> p k t", k=8)[:, :, 0:nb]
            P.op("dve", cp(nT[:, :, r0:r0 + nb], src), [B_ps[bank]], [nTbuf])

        def rope_T(src_ps, nb, cs_ap, out_bf, nheads, in_bufs, out_buf):
            cos = bass.AP(cs_ap.tensor, cs_ap.offset, [cs_ap.ap[0], [0, nheads], [1, 16]])
            sin = bass.AP(cs_ap.tensor, cs_ap.offset + 16, [cs_ap.ap[0], [0, nheads], [1, 16]])
            x1 = src_ps[:, :, 0:16]
            x2 = src_ps[:, :, 16:32]
            t = [rtmp[:nb, i, :] for i in range(4)]
            if nheads > 1:
                raise NotImplementedError
            t = [bass.AP(a.tensor, a.offset, [a.ap[0], [0, 1], [1, 16]]) for a in t]
            P.op("dve", tt(t[0], x1, cos, ALU.mult), in_bufs + [B_g], [B_rtmp])
            P.op("dve", tt(t[1], x2, sin, ALU.mult), in_bufs + [B_g], [B_rtmp])
            P.op("dve", tt(out_bf[:, :, 0:16], t[0], t[1], ALU.subtract), [B_rtmp], [out_buf])
            P.op("dve", tt(t[2], x2, cos, ALU.mult), in_bufs + [B_g], [B_rtmp])
            P.op("dve", tt(t[3], x1, sin, ALU.mult), in_bufs + [B_g], [B_rtmp])
            P.op("dve", tt(out_bf[:, :, 16:32], t[2], t[3], ALU.add), [B_rtmp], [out_buf])

        pending_ag = []

        def process_group(src_d, row0, ntok, kcol, wcol, csk_blk0, csq_blk0):
            gst = {}

            def g_alloc():
                gst["nT"], gst["nTb"] = nT_r.next()
            pipe1.unit([g_alloc])
            for bi, (r0, nb) in enumerate(blocks_of(ntok)):
                def mk(bi=bi, r0=r0, nb=nb):
                    st = {}
                    src_rows = src_d.ap()[row0 + r0:row0 + r0 + nb, :]

                    def s0():
                        xt, xb = xt_r.next()
                        st["xt"], st["xb"] = xt, xb
                        P.dma("sp", dmaf(xt[:nb], src_rows), [], [xb])

                    def s1():
                        xt, xb = st["xt"], st["xb"]
                        s_ap, s_b = smalls.next()
                        st["ssq"] = (s_ap, s_b)
                        P.op("act", act(junk[:nb], xt[:nb], AF.Square, accum_out=s_ap[:nb]), [xb], [B_junk, s_b])

                    def s2():
                        xt, xb = st["xt"], st["xb"]
                        s_ap, s_b = st["ssq"]
                        r_ap, r_b = rstd_from(s_ap, s_b, nb, 1.0 / D)
                        nbt, nbb = nb_r.next()
                        st["nbt"], st["nbb"] = nbt, nbb
                        P.op("dve", stt(nbt[:nb], xt[:nb], r_ap[:nb], gin[:nb], ALU.mult, ALU.mult), [xb, r_b, B_g], [nbb])

                    def s3():
                        nbt, nbb = st["nbt"], st["nbb"]
                        bank = tp_r.next()
                        st["tb0"] = bank

                        def trs(e):
                            ins = None
                            for kc in range(8):
                                ins = e.transpose(psb[:, bank, kc * 128:kc * 128 + nb], nbt[:nb, kc * 128:(kc + 1) * 128], ident[:nb, :nb])
                            return ins
                        P.op("pe", trs, [nbb] + CONST, [B_ps[bank]])

                    def s4():
                        bank = st["tb0"]
                        nT, nTbuf = gst["nT"], gst["nTb"]
                        src = psb[:, bank, :].rearrange("p (k t) -> p k t", k=8)[:, :, 0:nb]
                        P.op("act", lambda e: e.activation(nT[:, :, r0:r0 + nb], src, AF.Copy), [B_ps[bank]], [nTbuf])

                    def s5():
                        nT, nTbuf = gst["nT"], gst["nTb"]
                        if kcol is not None:
                            bank = lt_r.next()
                            st["kvb"] = bank
                            P.op("pe", mm_group(ps[:nb, bank, 0:288], [(nT[:, kc, r0:r0 + nb], win[:, kc, 1408:1696]) for kc in range(8)]),
                                 [nTbuf, B_win], [B_ps[bank]])
                        if wcol is not None:
                            bank = lt_r.next()
                            st["qb"] = bank
                            P.op("pe", mm_group(ps[:nb, bank, 0:384], [(nT[:, kc, r0:r0 + nb], win[:, kc, 1024:1408]) for kc in range(8)]),
                                 [nTbuf, B_win], [B_ps[bank]])

                    def s6():
                        if kcol is not None:
                            bank = st["kvb"]
                            s_ap, s_b = smalls.next()
                            P.op("act", act(junk[:nb, 0:256], ps[:nb, bank, 0:256], AF.Square, accum_out=s_ap[:nb]), [B_ps[bank]], [B_junk, s_b])
                            r_ap, r_b = rstd_from(s_ap, s_b, nb, 1.0 / 256)
                            lt, ltb = lat_r.next()
                            st["kvl"] = (lt, ltb)
                            P.op("dve", stt(lt[:nb, 0:256], ps[:nb, bank, 0:256], r_ap[:nb], gkv[:nb], ALU.mult, ALU.mult), [B_ps[bank], r_b, B_g], [ltb])
                            kr, krb = krot_r.next()
                            st["kr"] = (kr, krb)
                            src3 = ps[:nb, bank, 256:288].rearrange("p (h d) -> p h d", h=1)
                            out3 = kr[:nb, :].rearrange("p (h d) -> p h d", h=1)
                            rope_T(src3, nb, csk[:nb, csk_blk0 + bi, :], out3, 1, [B_ps[bank]], krb)
                        if wcol is not None:
                            bank = st["qb"]
                            s_ap, s_b = smalls.next()
                            P.op("act", act(junk[:nb, 0:384], ps[:nb, bank, 0:384], AF.Square, accum_out=s_ap[:nb]), [B_ps[bank]], [B_junk, s_b])
                            r_ap, r_b = rstd_from(s_ap, s_b, nb, 1.0 / 384)
                            lt, ltb = lat_r.next()
                            st["ql"] = (lt, ltb)
                            P.op("dve", stt(lt[:nb, 0:384], ps[:nb, bank, 0:384], r_ap[:nb], gq[:nb], ALU.mult, ALU.mult), [B_ps[bank], r_b, B_g], [ltb])

                    def s8():
                        if kcol is not None:
                            lt, ltb = st["kvl"]
                            kr, krb = st["kr"]
                            tb = tpb_r.next()
                            st["kvt"] = tb

                            def trs(e, lt=lt, kr=kr, tb=tb):
                                e.transpose(psb[:, tb, 0:nb], lt[:nb, 0:128], ident[:nb, :nb])
                                e.transpose(psb[:, tb, 128:128 + nb], lt[:nb, 128:256], ident[:nb, :nb])
                                return e.transpose(psb[0:32, tb, 256:256 + nb], kr[:nb, 0:32], ident[:nb, :nb])
                            P.op("pe", trs, [ltb, krb] + CONST, [B_ps[tb]])
                        if wcol is not None:
                            lt, ltb = st["ql"]
                            tb = tpb_r.next()
                            st["qt"] = tb

                            def trs2(e, lt=lt, tb=tb):
                                ins = None
                                for k in range(3):
                                    ins = e.transpose(psb[:, tb, k * 128:k * 128 + nb], lt[:nb, k * 128:(k + 1) * 128], ident[:nb, :nb])
                                return ins
                            P.op("pe", trs2, [ltb] + CONST, [B_ps[tb]])

                    def s9():
                        if kcol is not None:
                            tb = st["kvt"]
                            c0 = kcol + r0
                            srck = psb[:, tb, 0:256].rearrange("p (k t) -> p k t", k=2)[:, :, 0:nb]
                            P.op("act", lambda e, srck=srck, c0=c0: e.activation(ckvnT[:, :, c0:c0 + nb], srck, AF.Copy), [B_ps[tb]], [B_lat])
                            P.op("act", lambda e, tb=tb, c0=c0: e.activation(krotT[0:32, c0:c0 + nb], psb[0:32, tb, 256:256 + nb], AF.Copy), [B_ps[tb]], [B_lat])
                        if wcol is not None:
                            tb = st["qt"]
                            c1 = wcol + r0
                            srcq = psb[:, tb, 0:384].rearrange("p (k t) -> p k t", k=3)[:, :, 0:nb]
                            P.op("act", lambda e, srcq=srcq, c1=c1: e.activation(cqnT[:, :, c1:c1 + nb], srcq, AF.Copy), [B_ps[tb]], [B_lat])
                    pipe1.unit([s0, s1, s2, s3, s4, s5, s6, s8, s9])
                mk()
                if pending_ag:
                    pipe1.unit(pending_ag.pop(0))
            if wcol is not None:
                wt = wtile(wcol)
                for i in range(4):
                    def mk2(i=i):
                        st = {}

                        def a0():
                            nT, nTbuf = gst["nT"], gst["nTb"]
                            ba, bg = ag_r.next()
                            st["b"] = (ba, bg)
                            P.op("pe", mm_group(ps[:, ba, 0:ntok], [(win[:, kc, i * 128:(i + 1) * 128], nT[:, kc, 0:ntok]) for kc in range(8)]),
                                 [nTbuf, B_win], [B_ps[ba]])
                            P.op("pe", mm_group(ps[:, bg, 0:ntok], [(win[:, kc, 512 + i * 128:512 + (i + 1) * 128], nT[:, kc, 0:ntok]) for kc in range(8)]),
                                 [nTbuf, B_win], [B_ps[bg]])

                        def a1():
                            ba, bg = st["b"]
                            sg, sgb = sig_r.next()
                            P.op("act", act(sg[:, 0:ntok], ps[:, bg, 0:ntok], AF.Sigmoid), [B_ps[bg]], [sgb])
                            P.op("dve", tt(uT[:, i, UPAD + wcol:UPAD + wcol + ntok], ps[:, ba, 0:ntok], sg[:, 0:ntok], ALU.mult),
                                 [B_ps[ba], sgb, B_upad], [B_uT[wt]])
                        pending_ag.append([None, None, None, None, a0, a1])
                    mk2()

        def prefix_group(g):
            process_group(xp_d, g * 512, 512, g * 512, None, g * 4, None)
            pipe1.unit([diag_unit(g), late_loads(g)])
        if stop >= 1:
            for g in range(3):
                prefix_group(g)
            process_group(meta_d, 0, NMETA, NPRE, None, 16, None)
            process_group(xw_d, 0, HALO, None, 0, None, 0)
            for g in range(4):
                process_group(xw_d, HALO + g * 512, 512, KOWN + g * 512, HALO + g * 512, 17 + g * 4, 1 + g * 4)
            prefix_group(3)
        while pending_ag:
            pipe1.unit(pending_ag.pop(0))
        pipe1.emit()

        if debug and "ckvnT" in debug:
            P.dma("sp", dmaf(dbg_d["ckvnT"].ap(), ckvnT[:]), [B_lat], [])
        if debug and "krotT" in debug:
            P.dma("sp", dmaf(dbg_d["krotT"].ap(), krotT[:]), [B_lat], [])
        if debug and "cqnT" in debug:
            P.dma("sp", dmaf(dbg_d["cqnT"].ap(), cqnT[:]), [B_lat], [])
        if debug and "uT" in debug:
            P.dma("sp", dmaf(dbg_d["uT"].ap(), uT[:]), B_uT + [B_upad], [])

        P.barrier()
        A.lo = mark1
        ysets = []
        for k in range(2):
            ysets.append(dict(y=sb1("y_sb%d" % k, [128, 4, 512]), ybf=sb1("ybf%d" % k, [128, 4, 512], BF),
                              ysq=sb1("ysq%d" % k, [128, 4, 512], BF), stat=sb1("stat%d" % k, [128, 3, 512]),
                              By=[Buf() for _ in range(4)], Bybf=Buf(), Bysq=Buf(), Bstat=Buf()))
        tmp2 = [sb1("tmp2_%d" % i, [128, 512]) for i in range(2)]
        tmp_r = Rot([(tmp2[i], Buf()) for i in range(2)])
        cv_r = Rot([0, 1, 2, 3])
        stb_r = Rot([(4, 5), (6, 7)])
        pipe2 = Pipe()
        for ti, (c0, n) in enumerate([(0, HALO)] + [(HALO + g * 512, 512) for g in range(4)] if stop >= 2 else []):
            wt = wtile(c0)
            ys = ysets[ti % 2]
            y_sb, ybf, ysq, stat = ys["y"], ys["ybf"], ys["ysq"], ys["stat"]
            B_y, B_ybf, B_ysq, B_stat = ys["By"], ys["Bybf"], ys["Bysq"], ys["Bstat"]
            for i in range(4):
                def mkc(i=i, c0=c0, n=n, wt=wt, y_sb=y_sb, ybf=ybf, ysq=ysq, B_y=B_y, B_ybf=B_ybf, B_ysq=B_ysq):
                    st = {}

                    def c0_():
                        bank = cv_r.next()
                        st["b"] = bank
                        P.op("pe", mm_group(ps[:, bank, 0:n], [(diag[:, i, k, :], uT[:, i, c0 + k:c0 + k + n]) for k in range(31)]),
                             [B_diag, B_upad] + B_uT[max(0, wt - 1):wt + 1], [B_ps[bank]])

                    def c1_():
                        bank = st["b"]
                        cb = vecs[:, V_CB + i:V_CB + i + 1]
                        P.op("act", act(y_sb[:, i, 0:n], ps[:, bank, 0:n], AF.Identity, bias=cb), [B_ps[bank]] + CONST, [B_y[i]])
                        P.op("dve", cp(ybf[:, i, 0:n], y_sb[:, i, 0:n]), [B_y[i]], [B_ybf])
                        P.op("act", act(ysq[:, i, 0:n], y_sb[:, i, 0:n], AF.Square), [B_y[i]], [B_ysq])
                    pipe2.unit([c0_, c1_])
                mkc()

            def mks(n=n, ybf=ybf, ysq=ysq, stat=stat, B_ybf=B_ybf, B_ysq=B_ysq, B_stat=B_stat):
                mean = stat[:, 0, 0:n]
                m2 = stat[:, 1, 0:n]
                rs = stat[:, 2, 0:n]
                sst = {}

                def t0_():
                    bm, bq = stb_r.next()
                    sst["b"] = (bm, bq)
                    P.op("pe", mm_group(ps[:, bm, 0:n], [(onesm[:], ybf[:, i, 0:n]) for i in range(4)]), [B_ybf, B_c2], [B_ps[bm]])
                    P.op("pe", mm_group(ps[:, bq, 0:n], [(onesm[:], ysq[:, i, 0:n]) for i in range(4)]), [B_ysq, B_c2], [B_ps[bq]])

                def t1_():
                    bm, bq = sst["b"]
                    P.op("act", act(mean, ps[:, bm, 0:n], AF.Copy), [B_ps[bm]], [B_stat])
                    P.op("act", act(m2, ps[:, bm, 0:n], AF.Square), [B_ps[bm]], [B_stat])
                    P.op("act", act(rs, ps[:, bq, 0:n], AF.Copy), [B_ps[bq]], [B_stat])

                def t2_():
                    P.op("dve", tt(m2, rs, m2, ALU.subtract), [B_stat], [B_stat])
                    P.op("dve", ts(m2, m2, EPS, None, ALU.add), [B_stat], [B_stat])

                def t3_():
                    P.op("act", act(m2, m2, AF.Sqrt), [B_stat], [B_stat])

                def t4_():
                    P.op("dve", lambda e: e.reciprocal(rs, m2), [B_stat], [B_stat])
                pipe2.unit([None, t0_, t1_, t2_, t3_, t4_])
            mks()
            for i in range(4):
                def mka(i=i, c0=c0, n=n, wt=wt, y_sb=y_sb, stat=stat, B_y=B_y, B_stat=B_stat):
                    st = {}
                    mean = stat[:, 0, 0:n]
                    rs = stat[:, 2, 0:n]

                    def p0_():
                        t, tb_ = tmp_r.next()
                        st["t"] = (t, tb_)
                        P.op("dve", tt(t[:, 0:n], y_sb[:, i, 0:n], mean, ALU.subtract), [B_y[i], B_stat], [tb_])
                        P.op("dve", tt(t[:, 0:n], t[:, 0:n], rs, ALU.mult), [tb_, B_stat], [tb_])

                    def p1_():
                        t, tb_ = st["t"]
                        lg = vecs[:, V_LG + i:V_LG + i + 1]
                        lb = vecs[:, V_LB + i:V_LB + i + 1]
                        P.op("act", act(swT[:, i, c0:c0 + n], t[:, 0:n], AF.Silu, bias=lb, scale=lg), [tb_] + CONST, [B_swT[wt]])
                    pipe2.unit([None] * 5 + [p0_, p1_])
                mka()
        pipe2.emit()
        if debug and "swT" in debug:
            P.dma("sp", dmaf(dbg_d["swT"].ap(), swT[:]), B_swT, [])
        P.barrier()
        A.lo = mark_lat

    wout = A.down("wout", [128, 8, D], BF)
    oT = A.down("oT", [128, 4, NWIN], BF)
    B_wout = Buf()
    if True:
        sb3 = sb
        QT = sb3("QT", [128, 8, NWIN], BF)
        Vt = sb3("Vt", [128, 33, 8, 65], BF)
        KTs = [sb3("KT%d" % i, [128, NK], BF) for i in range(2)]
        PTs = [sb3("PT%d" % i, [128, 3, 512], BF) for i in range(2)]
        osbs = [sb3("osb%d" % i, [65, 512]) for i in range(2)]
        rden = sb3("rden", [65, 512])
        qbf = [sb3("qbf%d" % i, [128, 768], BF) for i in range(2)]
        csq3 = sb3("csq3", [128, 17, 128])
        rt3 = sb3("rt3", [128, 4, 4, 16])
        wstage = [sb3("wst%d" % i, [128, 512]) for i in range(2)]
        B_rden0 = Buf()
        P.op("pool", lambda e: e.memset(rden[:], 0.0), [], [B_rden0])

        B_csq3 = Buf()
        P.dma("sp", dmaf(csq3[:], csq4_d.ap()), [], [B_csq3])
        wst_r = Rot([(wstage[i], Buf()) for i in range(2)])
        for kc in range(8):
            for hf in range(2):
                st, stb = wst_r.next()
                P.dma("sp", dmaf(st[:], wout_d.ap()[:, kc, hf * 512:(hf + 1) * 512]), [], [stb])
                P.op("dve", ts(wout[:, kc, hf * 512:(hf + 1) * 512], st[:], vecs[:, V_GOUT + kc:V_GOUT + kc + 1], None, ALU.mult), [stb] + CONST, [B_wout])

        B_QT = Buf()
        B_QTm = Buf()
        B_V = Buf()
        B_Vone = Buf()
        B_KT = [Buf(), Buf()]
        B_KTc = [Buf(), Buf()]
        for h in range(8):
            P.dma("sp", dmaf(QT[96:105, h, :], maskq_d.ap()), [], [B_QTm])
        for i in range(2):
            P.dma("sp", dmaf(KTs[i][96:105, :], maskk_d.ap()), [], [B_KTc[i]])
            P.dma("sp", dmaf(KTs[i][64:96, :], krotT[:, :]), [B_lat], [B_KTc[i]])
        P.op("pool", lambda e: e.memset(Vt[:, :, :, 64:65], 1.0), [], [B_Vone])

        key_blocks = [(j * 128, 128) for j in range(16)] + [(NPRE, NMETA)] + [(KOWN + j * 128, 128) for j in range(16)]
        mb_r = Rot([6, 7])
        pipe3 = Pipe()
        for bi, (c0, nk) in enumerate(key_blocks if stop >= 2.5 else []):
            def mkv(bi=bi, c0=c0, nk=nk):
                st = {}

                def v0():
                    bank = mb_r.next()
                    st["b"] = bank
                    P.op("pe", mm_group(ps[:nk, bank, :], [(ckvnT[:, kc, c0:c0 + nk], wv[:, kc, :]) for kc in range(2)]), [B_lat, B_w3], [B_ps[bank]])

                def v1():
                    bank = st["b"]
                    src = ps[:nk, bank, :].rearrange("p (h d) -> p h d", h=8)
                    P.op("act", lambda e: e.activation(Vt[:nk, bi, :, 0:64], src, AF.Copy), [B_ps[bank], B_Vone], [B_V])
                pipe3.unit([v0, v1])
            mkv()
        win_blocks = [(0, HALO)] + [(HALO + j * 128, 128) for j in range(16)]
        qb_r = Rot([(qbf[i], Buf()) for i in range(2)])
        B_rt3 = Buf()
        q_r = Rot([(0, 1), (2, 3)])
        tq_r = Rot([4, 5])
        for bi, (c0, nb) in enumerate(win_blocks if stop >= 2.9 else []):
            def mkq(bi=bi, c0=c0, nb=nb):
                st = {}

                def q0():
                    b0, b1 = q_r.next()
                    st["bk"] = (b0, b1)
                    for half, bank in ((0, b0), (1, b1)):
                        P.op("pe", mm_group(ps[:nb, bank, 0:384], [(cqnT[:, kc, c0:c0 + nb], wuq[:, kc, half * 384:(half + 1) * 384]) for kc in range(3)]),
                             [B_lat, B_w3], [B_ps[bank]])

                def q1():
                    b0, b1 = st["bk"]
                    qb, qbb = qb_r.next()
                    st["qb"] = (qb, qbb)
                    for half, bank in ((0, b0), (1, b1)):
                        src = ps[:nb, bank, 0:384].rearrange("p (h d) -> p h d", h=4)
                        dst = qb[:nb, half * 384:(half + 1) * 384].rearrange("p (h d) -> p h d", h=4)
                        P.op("act", lambda e, src=src, dst=dst: e.activation(dst[:, :, 0:64], src[:, :, 0:64], AF.Copy), [B_ps[bank]], [qbb])
                        cos = csq3[:nb, bi, 0:64].rearrange("p (h d) -> p h d", h=4)
                        sin = csq3[:nb, bi, 64:128].rearrange("p (h d) -> p h d", h=4)
                        x1 = src[:, :, 64:80]
                        x2 = src[:, :, 80:96]
                        t = [rt3[:nb, i, 0:4, :] for i in range(4)]
                        P.op("dve", tt(t[0], x1, cos, ALU.mult), [B_ps[bank], B_csq3], [B_rt3])
                        P.op("dve", tt(t[1], x2, sin, ALU.mult), [B_ps[bank], B_csq3], [B_rt3])
                        P.op("dve", tt(dst[:, :, 64:80], t[0], t[1], ALU.subtract), [B_rt3], [qbb])
                        P.op("dve", tt(t[2], x2, cos, ALU.mult), [B_ps[bank], B_csq3], [B_rt3])
                        P.op("dve", tt(t[3], x1, sin, ALU.mult), [B_ps[bank], B_csq3], [B_rt3])
                        P.op("dve", tt(dst[:, :, 80:96], t[2], t[3], ALU.add), [B_rt3], [qbb])

                def q2():
                    qb, qbb = st["qb"]
                    tb = tq_r.next()
                    st["tb"] = tb

                    def trs(e):
                        ins = None
                        for h in range(8):
                            ins = e.transpose(psb[0:96, tb, h * 128:h * 128 + nb], qb[:nb, h * 96:(h + 1) * 96], ident[:nb, :nb])
                        return ins
                    P.op("pe", trs, [qbb] + CONST, [B_ps[tb]])

                def q3():
                    tb = st["tb"]
                    src = psb[0:96, tb, :].rearrange("p (h t) -> p h t", h=8)[:, :, 0:nb]
                    P.op("act", lambda e: e.activation(QT[0:96, :, c0:c0 + nb], src, AF.Copy), [B_ps[tb]], [B_QT])
                pipe3.unit([q0, q1, q2, q3])
            mkq()
        pipe3.emit()
        if debug and "QT" in debug:
            P.dma("sp", dmaf(dbg_d["QT"].ap(), QT[0:105, :, :]), [B_QT, B_QTm], [])
        if debug and "Vt" in debug:
            P.dma("sp", dmaf(dbg_d["Vt"].ap(), Vt[:]), [B_V, B_Vone], [])

        key_tiles = [(g * 512, 512) for g in range(4)] + [(NPRE, NMETA)] + [(KOWN + g * 512, 512) for g in range(4)]
        q_tiles = [(0, HALO, -1)] + [(HALO + g * 512, 512, g) for g in range(4)]
        S_r = Rot([(0, 1), (2, 3), (4, 5)])
        PT_r = Rot([(PTs[i], Buf()) for i in range(2)])
        osb_r = Rot([(osbs[i], Buf()) for i in range(2)])
        B_rden = Buf()
        OB = 6
        BB = 7
        pipe = Pipe()

        k_units = []

        def add_K(h):
            KT = KTs[h % 2]
            BK = B_KT[h % 2]
            for (c0, n) in key_tiles:
                stk = {}

                def s_mm(h=h, c0=c0, n=n, stk=stk):
                    bank = S_r.next()[0]
                    stk["bank"] = bank
                    P.op("pe", mm_group(ps[0:64, bank, 0:n], [(wk[:, kc, h * 64:(h + 1) * 64], ckvnT[:, kc, c0:c0 + n]) for kc in range(2)]),
                         [B_lat, B_w3], [B_ps[bank]])

                def s_cp(KT=KT, BK=BK, c0=c0, n=n, stk=stk):
                    bank = stk["bank"]
                    P.op("dve", cp(KT[0:64, c0:c0 + n], ps[0:64, bank, 0:n]), [B_ps[bank]], [BK])
                k_units.append([s_mm, s_cp])

        def add_group(h, qc, nq, grp, first, last):
            KT = KTs[h % 2]
            BK = B_KT[h % 2]
            st = {}
            gs = len(grp)
            nk = grp[0][1]
            qo = grp[0][4]

            def s_qk():
                banks = S_r.next()
                st["banks"] = banks

                def qk(e):
                    ins = None
                    for (kc0, nk_, K, bi, qo_), bank in zip(grp, banks):
                        ins = e.matmul(ps[:nk_, bank, qo_:nq], KT[0:K, kc0:kc0 + nk_], QT[0:K, h, qc + qo_:qc + nq], start=True, stop=True)
                    return ins
                P.op("pe", qk, [BK, B_KTc[h % 2], B_QT, B_QTm], [B_ps[b] for b in banks[:gs]])

            def s_exp():
                banks = st["banks"]
                pt, ptb = PT_r.next()
                st["pt"], st["ptb"] = pt, ptb
                b0 = banks[0]
                P.op("act", lambda e: e.activation(pt[:nk, 0:gs, qo:nq], ps[:nk, b0:b0 + gs, qo:nq], AF.Exp, scale=SCALE),
                     [B_ps[b] for b in banks[:gs]], [ptb])

            def s_pv():
                pt, ptb = st["pt"], st["ptb"]

                def pv(e):
                    ins = None
                    for i, (kc0, nk_, K, bi, qo_) in enumerate(grp):
                        ins = e.matmul(ps[0:65, OB, qo_:nq], Vt[:nk_, bi, h, :], pt[:nk_, i, qo_:nq],
                                       start=(first and i == 0), stop=(last and i == gs - 1))
                    return ins
                P.op("pe", pv, [ptb, B_V, B_Vone], [B_ps[OB]])
            pipe.unit([s_qk, None, s_exp, s_pv])

        def add_epilogue(h, qc, nq):
            st = {}

            def s_cp():
                osb, osbb = osb_r.next()
                st["osb"], st["osbb"] = osb, osbb
                P.op("dve", cp(osb[:, 0:nq], ps[0:65, OB, 0:nq]), [B_ps[OB]], [osbb])

            def s_rc():
                osb, osbb = st["osb"], st["osbb"]
                P.op("dve", lambda e: e.reciprocal(rden[64:65, 0:nq], osb[64:65, 0:nq]), [osbb, B_rden0], [B_rden])

            def s_bc():
                P.op("pe", mm_group(ps[0:64, BB, 0:nq], [(onesf[0:65, 0:64], rden[0:65, 0:nq])]), [B_rden, B_rden0, B_c2], [B_ps[BB]])

            def s_mul():
                osb, osbb = st["osb"], st["osbb"]
                po = (h % 2) * 64
                P.op("dve", tt(oT[po:po + 64, h // 2, qc:qc + nq], osb[0:64, 0:nq], ps[0:64, BB, 0:nq], ALU.mult),
                     [osbb, B_ps[BB]], [B_oT[wtile(qc)]])
            pipe.unit([None, None, None, None, s_cp, s_rc, None, None, None, s_bc, s_mul])

        if stop >= 4:
            add_K(0)
            while k_units:
                pipe.unit(k_units.pop(0))
        for h in range(8 if stop >= 4 else 0):
            for qi, (qc, nq, g) in enumerate(q_tiles):
                blks = [(j * 128, 128, 97, j, 0) for j in range(16)] + [(NPRE, NMETA, 96, 16, 0)]
                if g >= 0:
                    for kb in range(4 * g + 4):
                        dj = kb - 4 * g
                        blks.append((KOWN + kb * 128, 128, 105 if dj >= 0 else 96, 17 + kb, 128 * dj if dj > 0 else 0))
                groups = []
                cur = []
                for b in blks:
                    if cur and (len(cur) == 2 or cur[0][1] != b[1] or cur[0][4] != b[4]):
                        groups.append(cur)
                        cur = []
                    cur.append(b)
                if cur:
                    groups.append(cur)
                for gi, grp in enumerate(groups):
                    add_group(h, qc, nq, grp, gi == 0, gi == len(groups) - 1)
                    if k_units and gi % 2 == 1:
                        pipe.unit(k_units.pop(0))
                add_epilogue(h, qc, nq)
                if qi == 0 and h + 1 < 8:
                    add_K(h + 1)
                if qi == len(q_tiles) - 1:
                    while k_units:
                        pipe.unit(k_units.pop(0))
        pipe.emit()
        if debug and "oT" in debug:
            P.dma("sp", dmaf(dbg_d["oT"].ap(), oT[:]), B_oT, [])
        P.barrier()
    A.lo = mark_persist

    if True:
        sb5 = sb
        gfin = sb5("gfin", [128, D])
        gffn = sb5("gffn", [128, D])
        P.dma("sp", dmaf(gfin[:], gfin_d.ap()), [], [B_const])
        P.dma("sp", dmaf(gffn[:], gffn_d.ap()), [], [B_const])
        wdn = sb5("wdn", [128, 22, D], BF)
        h1 = sb5("h1", [128, 6, D])
        nT2 = sb5("nT2", [128, 8, 688], BF)
        actT = sb5("actT", [128, 22, 688], BF)
        wups = [sb5("wup%d" % i, [128, 8, 256], BF) for i in range(2)]
        xts = [sb5("xt5_%d" % i, [128, D]) for i in range(2)]
        sqs = [sb5("sq5_%d" % i, [128, 4, 128], BF) for i in range(2)]
        junk = sb5("junk5", [128, D], BF)
        n2s = [sb5("n2_%d" % i, [128, D], BF) for i in range(1)]
        accs = [sb5("acc%d" % i, [128, 512]) for i in range(4)]
        sgs = [sb5("sg%d" % i, [128, 512]) for i in range(2)]
        outs = [sb5("outs%d" % i, [128, D]) for i in range(1)]

        B_wdn = Buf()
        wdn_loaded = [False]
        B_actz = Buf()
        P.op("pool", lambda e: e.memset(actT[:, :, 0:2], 0.0), [], [B_actz])
        P.op("pool", lambda e: e.memset(actT[:, :, 512:514], 0.0), [], [B_actz])
        xt_r = Rot([(xts[i], Buf()) for i in range(2)])
        sq_r = Rot([(sqs[i], Buf()) for i in range(2)])
        n2_r = Rot([(n2s[i], Buf()) for i in range(1)])
        wup_r = Rot([(wups[i], Buf()) for i in range(2)])
        acc_r = Rot([(accs[i], Buf()) for i in range(4)])
        sg_r = Rot([(sgs[i], Buf()) for i in range(2)])
        out_r = Rot([(outs[i], Buf()) for i in range(1)])
        B_junk = Buf()
        B_h1 = [Buf() for _ in range(6)]
        B_nT2 = Buf()
        B_actT = Buf()
        st_r = Rot([6, 7])
        up_r = Rot([(0, 1), (2, 3)])
        dn_r = Rot([(4, 5)])
        tp_r = Rot([4, 5])

        up3_r = Rot([(0, 1), (2, 3), (4, 5)])
        pipe5 = None
        for macro in (FFN_MACROS if stop >= 5 else []):
            offs = []
            o = 0
            for (s, n) in macro:
                offs.append(o)
                o += n + 2
            blist = []
            for si, (s, n) in enumerate(macro):
                for (r0, nb) in blocks_of(n + 2):
                    blist.append((si, s - 2 + r0, nb, offs[si] + r0, len(blist)))
            if pipe5 is None:
                pipe5 = Pipe()
            else:
                pipe5.unit([None])
                pipe5.unit([None])
            pipe = pipe5
            for (si, tok0, nb, col, hs) in blist:
                def mk(si=si, tok0=tok0, nb=nb, col=col, hs=hs):
                    st = {}
                    srcs = [B_swT[t] for t in wtiles(tok0, nb)] + [B_oT[t] for t in wtiles(tok0, nb)]

                    def s0():
                        st["sq"] = []
                        for (srcT, Bsrc) in ((swT, B_swT), (oT, B_oT)):
                            sq, sqb = sq_r.next()
                            P.op("pool", tt(sq[:, :, 0:nb], srcT[:, :, tok0:tok0 + nb], srcT[:, :, tok0:tok0 + nb], ALU.mult),
                                 [Bsrc[t] for t in wtiles(tok0, nb)], [sqb])
                            st["sq"].append((sq, sqb))

                    def s1():
                        st["bk"] = []
                        for (sq, sqb) in st["sq"]:
                            bank = st_r.next()
                            P.op("pe", mm_group(ps[:nb, bank, 0:1], [(sq[:, i, 0:nb], onecol[:, 0:1]) for i in range(4)]), [sqb, B_c2], [B_ps[bank]])
                            st["bk"].append(bank)

                    def s2():
                        xt, xb = xt_r.next()
                        st["xt"], st["xb"] = xt, xb
                        P.dma("sp", dmaf(xt[:nb], xw_d.ap()[tok0:tok0 + nb, :]), [], [xb])
                        st["rst"] = []
                        for bank in st["bk"]:
                            s_ap, s_b = smalls.next()
                            P.op("act", act(s_ap[:nb], ps[:nb, bank, 0:1], AF.Copy), [B_ps[bank]], [s_b])
                            st["rst"].append(rstd_from(s_ap, s_b, nb, 1.0 / 512))

                    def s3():
                        ba = up_r.next()
                        bb = up_r.next()
                        st["ba"], st["bb"] = ba, bb
                        for half in range(2):
                            P.op("pe", mm_group(ps[:nb, ba[half], :], [(swT[:, i, tok0:tok0 + nb], wout[:, i, half * 512:(half + 1) * 512]) for i in range(4)]),
                                 srcs + [B_wout], [B_ps[ba[half]]])
                            P.op("pe", mm_group(ps[:nb, bb[half], :], [(oT[:, i, tok0:tok0 + nb], wout[:, 4 + i, half * 512:(half + 1) * 512]) for i in range(4)]),
                                 srcs + [B_wout], [B_ps[bb[half]]])

                    def s4():
                        ba, bb, xt, xb, rst = st["ba"], st["bb"], st["xt"], st["xb"], st["rst"]
                        for half in range(2):
                            hv = h1[:nb, hs, half * 512:(half + 1) * 512]
                            P.op("dve", stt(hv, ps[:nb, ba[half], :], rst[0][0][:nb], xt[:nb, half * 512:(half + 1) * 512], ALU.mult, ALU.add),
                                 [B_ps[ba[half]], rst[0][1], xb], [B_h1[hs]])
                            P.op("dve", stt(hv, ps[:nb, bb[half], :], rst[1][0][:nb], hv, ALU.mult, ALU.add),
                                 [B_ps[bb[half]], rst[1][1], B_h1[hs]], [B_h1[hs]])

                    def s5():
                        s_ap, s_b = smalls.next()
                        P.op("act", act(junk[:nb], h1[:nb, hs, :], AF.Square, accum_out=s_ap[:nb]), [B_h1[hs]], [B_junk, s_b])
                        r_ap, r_b = rstd_from(s_ap, s_b, nb, 1.0 / D)
                        n2, n2b = n2_r.next()
                        st["n2"], st["n2b"] = n2, n2b
                        P.op("dve", stt(n2[:nb], h1[:nb, hs, :], r_ap[:nb], gffn[:nb], ALU.mult, ALU.mult), [B_h1[hs], r_b] + CONST, [n2b])

                    def s6():
                        n2, n2b = st["n2"], st["n2b"]
                        tb = tp_r.next()
                        st["tb"] = tb

                        def trs(e):
                            ins = None
                            for kc in range(8):
                                ins = e.transpose(psb[:, tb, kc * 128:kc * 128 + nb], n2[:nb, kc * 128:(kc + 1) * 128], ident[:nb, :nb])
                            return ins
                        P.op("pe", trs, [n2b] + CONST, [B_ps[tb]])

                    def s7():
                        tb = st["tb"]
                        src = psb[:, tb, :].rearrange("p (k t) -> p k t", k=8)[:, :, 0:nb]
                        P.op("act", lambda e: e.activation(nT2[:, :, col:col + nb], src, AF.Copy), [B_ps[tb]], [B_nT2])
                    pipe.unit([s0, s1, s2, s3, s4, s5, s6, s7])
                mk()
            for j in range(22):
                wst = {}
                for si, (s, n) in enumerate(macro):
                    def mk(j=j, si=si, s=s, n=n, wst=wst):
                        st = {}
                        c0 = offs[si]

                        def s_dma():
                            wu, wub = wup_r.next()
                            wst["wu"], wst["wub"] = wu, wub
                            P.dma("pool", lambda e: e.dma_start(wu[:], wup_d.ap()[j], max_dma_last_dim=8192), [], [wub])
                            if not wdn_loaded[0]:
                                load_cast(wdn[:, j, :], wdn_d.ap()[:, j, :], B_wdn)
                                if j == 21:
                                    wdn_loaded[0] = True

                        def s_mm():
                            wu, wub = wst["wu"], wst["wub"]
                            bg, bv = up3_r.next()
                            st["bg"], st["bv"] = bg, bv
                            P.op("pe", mm_group(ps[:, bg, 0:n + 2], [(wu[:, kc, 0:128], nT2[:, kc, c0:c0 + n + 2]) for kc in range(8)]), [wub, B_nT2], [B_ps[bg]])
                            P.op("pe", mm_group(ps[:, bv, 0:n + 2], [(wu[:, kc, 128:256], nT2[:, kc, c0:c0 + n + 2]) for kc in range(8)]), [wub, B_nT2], [B_ps[bv]])

                        def taps(cc):
                            w0 = vecs[:, V_FW + cc * 3 + 0:V_FW + cc * 3 + 1]
                            w1 = vecs[:, V_FW + cc * 3 + 1:V_FW + cc * 3 + 2]
                            w2 = vecs[:, V_FW + cc * 3 + 2:V_FW + cc * 3 + 3]
                            fb = vecs[:, V_FB + cc:V_FB + cc + 1]
                            return w0, w1, w2, fb

                        def s_act():
                            st["res"] = []
                            for (bank, cc) in ((st["bg"], j), (st["bv"], 22 + j)):
                                ac, acb = acc_r.next()
                                w0, w1, w2, fb = taps(cc)
                                P.op("act", act(ac[:, 0:n], ps[:, bank, 2:n + 2], AF.Identity, bias=fb, scale=w2), [B_ps[bank]] + CONST, [acb])
                                st["res"].append((ac, acb))

                        def s_dve():
                            for (bank, cc), (ac, acb) in zip(((st["bg"], j), (st["bv"], 22 + j)), st["res"]):
                                w0, w1, w2, fb = taps(cc)
                                P.op("dve", stt(ac[:, 0:n], ps[:, bank, 1:n + 1], w1, ac[:, 0:n], ALU.mult, ALU.add), [B_ps[bank], acb] + CONST, [acb])
                                P.op("dve", stt(ac[:, 0:n], ps[:, bank, 0:n], w0, ac[:, 0:n], ALU.mult, ALU.add), [B_ps[bank], acb] + CONST, [acb])

                        def s_gate():
                            res = st["res"]
                            sg, sgb = sg_r.next()
                            P.op("act", act(sg[:, 0:n], res[0][0][:, 0:n], AF.Silu), [res[0][1]], [sgb])
                            P.op("pool", tt(actT[:, j, c0 + 2:c0 + 2 + n], sg[:, 0:n], res[1][0][:, 0:n], ALU.mult), [sgb, res[1][1], B_actz], [B_actT])
                        pipe.unit([None, None, None, None, s_dma if si == 0 else None, None, s_mm, s_act, s_dve, s_gate])
                    mk()
            for (si, tok0, nb, col, hs) in blist:
                def mk(si=si, tok0=tok0, nb=nb, col=col, hs=hs):
                    st = {}
                    s, n = macro[si]

                    def d0():
                        b2 = dn_r.next()
                        st["b2"] = b2
                        for half in range(2):
                            P.op("pe", mm_group(ps[:nb, b2[half], :], [(actT[:, j, col:col + nb], wdn[:, j, half * 512:(half + 1) * 512]) for j in range(22)]),
                                 [B_actT, B_actz, B_wdn], [B_ps[b2[half]]])

                    def d1():
                        b2 = st["b2"]
                        for half in range(2):
                            hv = h1[:nb, hs, half * 512:(half + 1) * 512]
                            P.op("dve", tt(hv, ps[:nb, b2[half], :], hv, ALU.add), [B_ps[b2[half]], B_h1[hs]], [B_h1[hs]])

                    def d2():
                        s_ap, s_b = smalls.next()
                        P.op("act", act(junk[:nb], h1[:nb, hs, :], AF.Square, accum_out=s_ap[:nb]), [B_h1[hs]], [B_junk, s_b])
                        st["r"] = rstd_from(s_ap, s_b, nb, 1.0 / D)

                    def d3():
                        r_ap, r_b = st["r"]
                        ot, otb = out_r.next()
                        st["ot"], st["otb"] = ot, otb
                        P.op("dve", stt(ot[:nb], h1[:nb, hs, :], r_ap[:nb], gfin[:nb], ALU.mult, ALU.mult), [B_h1[hs], r_b] + CONST, [otb])

                    def d4():
                        ot, otb = st["ot"], st["otb"]
                        lo = max(tok0, s)
                        hi = tok0 + nb
                        if hi > lo:
                            P.dma("sp", dmaf(out_d.ap()[lo - HALO:hi - HALO, :], ot[lo - tok0:nb, :]), [otb], [])
                    pipe.unit([None] * 8 + [d0, d1, d2, d3, d4])
                mk()
        if pipe5 is not None:
            pipe5.emit()
        P.barrier()

    sems = {}
    for k in sorted(P.semkeys, key=str):
        sems[k] = es.enter_context(nc.semaphore("s_" + "_".join(str(x) for x in (k if isinstance(k, tuple) else (k,)))))
    engobj = {"pe": "tensor", "act": "scalar", "dve": "vector", "pool": "gpsimd", "sp": "sync"}
    with nc.Block() as block:
        for ename in Prog.ENG:
            def body(e, ename=ename):
                for (waits, fn, inc) in P.q[ename]:
                    for (k, v) in waits:
                        e.wait_ge(sems[k], v)
                    if fn is None:
                        continue
                    ins = fn(e)
                    ins.then_inc(sems[inc[0]], inc[1])
            getattr(block, engobj[ename])(body)
    es.close()
    return nc


def _rows(v, n):
    return np.ascontiguousarray(np.asarray(v, np.float32).reshape(n, 128).T)


def _prep_shared(inp):
    f = lambda a: np.asarray(a, np.float32)
    sh = {}
    sh["meta"] = np.ascontiguousarray(f(inp["meta_tokens"]))
    sh["ident"] = np.eye(128, dtype=np.float32).astype(ml_dtypes.bfloat16)
    vecs = np.zeros((128, NV), np.float32)
    gout = np.concatenate([f(inp["conv_out_g"])[0], f(inp["attn_out_g"])[0]])
    vecs[:, V_GOUT:V_GOUT + 8] = _rows(gout, 8)
    cw = f(inp["conv_w"])[0]
    vecs[:, V_CW:V_CW + 124] = cw.reshape(31, 4, 128).transpose(2, 1, 0).reshape(128, 124)
    vecs[:, V_CB:V_CB + 4] = _rows(f(inp["conv_b"])[0], 4)
    vecs[:, V_LG:V_LG + 4] = _rows(f(inp["conv_ln_g"])[0], 4)
    vecs[:, V_LB:V_LB + 4] = _rows(f(inp["conv_ln_b"])[0], 4)
    fw = f(inp["ffn_conv_w"])[0]
    vecs[:, V_FW:V_FW + 132] = fw.reshape(3, 44, 128).transpose(2, 1, 0).reshape(128, 132)
    vecs[:, V_FB:V_FB + 44] = _rows(f(inp["ffn_conv_b"])[0], 44)
    sh["vecs"] = vecs
    bc = lambda v: np.ascontiguousarray(np.broadcast_to(f(v).reshape(1, -1), (128, f(v).size)))
    sh["gin"] = bc(inp["mix_norm_g"][0])
    sh["gq"] = bc(inp["q_norm_g"][0])
    sh["gkv"] = bc(inp["kv_norm_g"][0])
    sh["gffn"] = bc(inp["ffn_norm_g"][0])
    sh["gfin"] = bc(inp["final_norm_g"])
    sh["win"] = np.ascontiguousarray(f(inp["w_in"])[0].reshape(8, 128, 1696).transpose(1, 0, 2))
    sh["wuq"] = np.ascontiguousarray(f(inp["w_uq"])[0].reshape(3, 128, 768).transpose(1, 0, 2))
    wukv = f(inp["w_ukv"])[0].reshape(2, 128, 8, 2, 64)
    sh["wk"] = np.ascontiguousarray(wukv[:, :, :, 0, :].transpose(1, 0, 2, 3).reshape(128, 2, 512))
    sh["wv"] = np.ascontiguousarray(wukv[:, :, :, 1, :].transpose(1, 0, 2, 3).reshape(128, 2, 512))
    sh["wout"] = np.ascontiguousarray(f(inp["w_out"])[0].reshape(8, 128, 1024).transpose(1, 0, 2))
    wup = f(inp["w_ffn_up"])[0].reshape(8, 128, 2, 22, 128)
    sh["wup"] = np.ascontiguousarray(wup.transpose(3, 1, 0, 2, 4).reshape(22, 128, 8, 256))
    sh["wdn"] = np.ascontiguousarray(f(inp["w_ffn_down"])[0].reshape(22, 128, 1024).transpose(1, 0, 2))
    mk = np.zeros((9, NK), np.float32)
    mk[0, :NPRE] = 1.0
    ko = np.arange(NOWN)
    lk = (ko % 512) // 64
    for j in range(8):
        mk[1 + j, KOWN + ko[lk == j]] = 1.0
    sh["maskk"] = mk.astype(ml_dtypes.bfloat16)
    return sh


def _cs_table(pos, nblk_rows):
    inv = 1.0 / (10000.0 ** (np.arange(0, 32, 2, dtype=np.float32) / np.float32(32)))
    inv = inv.astype(np.float32)
    out = np.zeros((128, len(pos), 32), np.float32)
    for b, p in enumerate(pos):
        ang = (np.asarray(p, np.float32)[:, None] * inv[None, :]).astype(np.float32)
        out[:len(p), b, 0:16] = np.cos(ang)
        out[:len(p), b, 16:32] = np.sin(ang)
    return out


def _prep_core(inp, b, s):
    x = np.asarray(inp["x"], np.float32)[b]
    meta = np.asarray(inp["meta_tokens"], np.float32)
    m = {}
    xw = np.zeros((NWIN, D), np.float32)
    if s == 0:
        xw[16:32] = meta
        xw[32:] = x[0:2048]
        xp = np.zeros((NPRE, D), np.float32)
        pos_pre = np.zeros(NPRE)
        pos_halo = np.concatenate([np.zeros(16), np.arange(16)])
        pos_own = 16 + np.arange(2048)
    else:
        xw[:] = x[2016:4096]
        xp = np.ascontiguousarray(x[0:2048])
        pos_pre = 16 + np.arange(2048)
        pos_halo = 16 + 2016 + np.arange(32)
        pos_own = 16 + 2048 + np.arange(2048)
    m["xw"] = xw
    m["xp"] = xp
    kpos = [pos_pre[i * 128:(i + 1) * 128] for i in range(16)] + [np.arange(16)] + [pos_own[i * 128:(i + 1) * 128] for i in range(16)]
    m["csk"] = _cs_table(kpos, 33)
    qpos = [pos_halo] + [pos_own[i * 128:(i + 1) * 128] for i in range(16)]
    m["csq"] = _cs_table(qpos, 17)
    c4 = np.zeros((128, 17, 128), np.float32)
    c4[:, :, 0:64] = np.tile(m["csq"][:, :, 0:16], (1, 1, 4))
    c4[:, :, 64:128] = np.tile(m["csq"][:, :, 16:32], (1, 1, 4))
    m["csq4"] = c4
    mq = np.zeros((9, NWIN), np.float32)
    mq[0, :] = 0.0 if s == 1 else NEGM
    qo = np.arange(NOWN)
    lq = (qo % 512) // 64
    for j in range(8):
        mq[1 + j, HALO + qo[lq < j]] = NEGM
    m["maskq"] = mq.astype(ml_dtypes.bfloat16)
    return m


_NC_CACHE = {}


def kernel(**inputs):
    debug = inputs.pop("_debug", None)
    stop = inputs.pop("_stop", 99)
    key = (tuple(sorted(debug.items())) if debug else None, stop)
    if key not in _NC_CACHE:
        _NC_CACHE[key] = build(debug, stop)
    nc = _NC_CACHE[key]
    sh = _prep_shared(inputs)
    in_maps = []
    for c in range(8):
        m = dict(sh)
        m.update(_prep_core(inputs, c // 2, c % 2))
        in_maps.append(m)
    res = run_bass_kernel_spmd(nc, in_maps, core_ids=list(range(8)))
    out = np.zeros((4, 4096, D), np.float32)
    for c in range(8):
        out[c // 2, (c % 2) * 2048:(c % 2 + 1) * 2048] = np.asarray(res.results[c]["out"], np.float32)
    if debug:
        kernel.last_debug = [{k: np.asarray(v) for k, v in r.items() if k.startswith("dbg_")} for r in res.results]
    return out
```

```python
import numpy as np
import ml_dtypes
from contextlib import ExitStack
import concourse.bass as bass
import concourse.mybir as mybir
from concourse.bass_utils import run_bass_kernel_spmd

F32 = mybir.dt.float32
BF = mybir.dt.bfloat16
AF = mybir.ActivationFunctionType
ALU = mybir.AluOpType

D = 1024
NWIN = 2080
HALO = 32
NOWN = 2048
NPRE = 2048
NMETA = 16
NK = NPRE + NMETA + NOWN
KOWN = NPRE + NMETA
EPS = 1e-6
SCALE = 96 ** -0.5
NEGM = -30000.0
UPAD = 30
V_GOUT = 0
V_CW = 8
V_CB = V_CW + 124
V_LG = V_CB + 4
V_LB = V_LG + 4
V_FW = V_LB + 4
V_FB = V_FW + 132
NV = V_FB + 44

FFN_MACROS = [[(32, 510), (542, 173)], [(715, 510), (1225, 173)], [(1398, 510), (1908, 172)]]


class Buf:
    __slots__ = ("w", "r", "x")

    def __init__(self, excl=False):
        self.w = None
        self.r = {}
        self.x = excl


class Prog:
    ENG = ("pe", "act", "dve", "pool", "sp")
    NSLOT = 8

    def __init__(self):
        self.q = {e: [] for e in self.ENG}
        self.cnt = {e: 0 for e in ("pe", "act", "dve", "pool")}
        self.seen = {e: {} for e in self.ENG}
        self.dma_n = {e: 0 for e in self.ENG}
        self.semkeys = set(["pe", "act", "dve", "pool"])

    def _waits(self, eng, reads, writes, extra=()):
        deps = {}

        def add(k, v):
            if v > deps.get(k, 0):
                deps[k] = v
        for b in reads:
            if b.w:
                add(*b.w)
            if b.x:
                for k, v in b.r.items():
                    if k != eng:
                        add(k, v)
        for b in writes:
            if b.w:
                add(*b.w)
            for k, v in b.r.items():
                add(k, v)
        for k, v in extra:
            add(k, v)
        waits = []
        for k, v in deps.items():
            if k == eng and eng == "pe":
                continue
            if self.seen[eng].get(k, 0) < v:
                waits.append((k, v))
                self.seen[eng][k] = v
        return waits

    def op(self, eng, fn, reads=(), writes=()):
        waits = self._waits(eng, reads, writes)
        self.cnt[eng] += 1
        c = self.cnt[eng]
        self.q[eng].append((waits, fn, (eng, 1)))
        for b in reads:
            if b.r.get(eng, 0) < c:
                b.r[eng] = c
        for b in writes:
            b.w = (eng, c)
            b.r = {}

    def dma(self, queue, fn, reads=(), writes=()):
        i = self.dma_n[queue]
        self.dma_n[queue] += 1
        key = ("dma", queue, i % self.NSLOT)
        self.semkeys.add(key)
        prev = 16 * (i // self.NSLOT)
        target = prev + 16
        waits = self._waits(queue, reads, writes, extra=((key, prev),) if prev else ())
        self.q[queue].append((waits, fn, (key, 16)))
        for b in reads:
            b.r[key] = target
        for b in writes:
            b.w = (key, target)
            b.r = {}

    def barrier(self):
        allk = {k: v for k, v in self.cnt.items()}
        for q in self.ENG:
            n = self.dma_n[q]
            for sl in range(min(n, self.NSLOT)):
                cntsl = (n - sl + self.NSLOT - 1) // self.NSLOT
                allk[("dma", q, sl)] = 16 * cntsl
        for e in self.ENG:
            waits = []
            for k, v in allk.items():
                if v and self.seen[e].get(k, 0) < v and not (k == e and e == "pe"):
                    waits.append((k, v))
                    self.seen[e][k] = v
            if waits:
                self.q[e].append((waits, None, None))


class Arena:
    def __init__(self, nc, lo=16512, hi=229344):
        self.nc = nc
        self.lo = lo
        self.hi = hi
        self.n = 0

    def _size(self, shape, dt):
        n = 1
        for x in shape[1:]:
            n *= x
        b = n * (2 if dt == BF else 4)
        return (b + 31) // 32 * 32

    def up(self, name, shape, dt=F32):
        sz = self._size(shape, dt)
        off = self.lo
        self.lo += sz
        assert self.lo <= self.hi, ("SBUF overflow", name, self.lo, self.hi)
        self.n += 1
        return self.nc.alloc_sbuf_tensor_at("%s_%d" % (name, self.n), list(shape), dt, offset=off)

    def down(self, name, shape, dt=F32):
        sz = self._size(shape, dt)
        self.hi -= sz
        assert self.lo <= self.hi, ("SBUF overflow", name, self.lo, self.hi)
        self.n += 1
        return self.nc.alloc_sbuf_tensor_at("%s_%d" % (name, self.n), list(shape), dt, offset=self.hi)


class Pipe:
    def __init__(self):
        self.units = []

    def unit(self, stages, lead=0):
        self.units.append([None] * lead + list(stages))

    def emit(self):
        U = len(self.units)
        if not U:
            return
        S = max(len(u) for u in self.units)
        for t in range(U + S):
            for u in range(max(0, t - S + 1), min(U - 1, t) + 1):
                st = self.units[u]
                k = t - u
                if k < len(st) and st[k] is not None:
                    st[k]()
        self.units = []


class Rot:
    def __init__(self, items):
        self.items = items
        self.i = 0

    def next(self):
        x = self.items[self.i % len(self.items)]
        self.i += 1
        return x


def blocks_of(n, bs=128):
    out = []
    r = 0
    while r < n:
        out.append((r, min(bs, n - r)))
        r += bs
    return out


def build(debug=None, stop=99):
    nc = bass.Bass("TRN2", target_bir_lowering=False)
    P = Prog()
    es = ExitStack()

    def dram(name, shape, dt=F32, out=False):
        return nc.dram_tensor(name, list(shape), dt, kind="ExternalOutput" if out else "ExternalInput")

    xw_d = dram("xw", [NWIN, D])
    xp_d = dram("xp", [NPRE, D])
    meta_d = dram("meta", [NMETA, D])
    csk_d = dram("csk", [128, 33, 32])
    csq_d = dram("csq", [128, 17, 32])
    csq4_d = dram("csq4", [128, 17, 128])
    maskk_d = dram("maskk", [9, NK], BF)
    maskq_d = dram("maskq", [9, NWIN], BF)
    ident_d = dram("ident", [128, 128], BF)
    vecs_d = dram("vecs", [128, NV])
    gin_d = dram("gin", [128, D])
    gq_d = dram("gq", [128, 384])
    gkv_d = dram("gkv", [128, 256])
    gffn_d = dram("gffn", [128, D])
    gfin_d = dram("gfin", [128, D])
    win_d = dram("win", [128, 8, 1696])
    wuq_d = dram("wuq", [128, 3, 768])
    wk_d = dram("wk", [128, 2, 512])
    wv_d = dram("wv", [128, 2, 512])
    wout_d = dram("wout", [128, 8, D])
    wup_d = dram("wup", [22, 128, 8, 256])
    wdn_d = dram("wdn", [128, 22, D])
    out_d = dram("out", [NOWN, D], out=True)
    dbg_d = {}
    if debug:
        for nm, shp in debug.items():
            dbg_d[nm] = dram("dbg_" + nm, shp, out=True)

    A = Arena(nc)

    def sb(name, shape, dt=F32):
        return A.up(name, shape, dt)

    ident = sb("ident", [128, 128], BF)
    onesm = sb("onesm", [128, 128], BF)
    onecol = sb("onecol", [128, 1], BF)
    onesf = sb("onesf", [128, 64], F32)
    vecs = sb("vecs", [128, NV])
    swT = sb("swT", [128, 4, NWIN], BF)
    small = sb("small", [128, 64])
    ps = es.enter_context(nc.psum_tensor("ps", [128, 8, 512], F32))
    psb = ps.bitcast(BF)
    B_ps = [Buf(True) for _ in range(8)]
    B_const = Buf()
    B_swT = [Buf() for _ in range(5)]
    B_oT = [Buf() for _ in range(5)]
    smalls = Rot([(small[:, i:i + 1], Buf()) for i in range(64)])

    def wtile(col):
        return 0 if col < HALO else 1 + (col - HALO) // 512

    def wtiles(c0, n):
        return sorted(set([wtile(c0), wtile(c0 + n - 1)]))

    def act(out, in_, func, bias=0.0, scale=1.0, accum_out=None):
        if accum_out is not None:
            return lambda e: e.activation(out, in_, func, bias=bias, scale=scale, accum_out=accum_out)
        return lambda e: e.activation(out, in_, func, bias=bias, scale=scale)

    def tt(out, a, b, op):
        return lambda e: e.tensor_tensor(out, a, b, op)

    def ts(out, a, s1, s2, op0, op1=None):
        if op1 is None:
            return lambda e: e.tensor_scalar(out, a, s1, None, op0)
        return lambda e: e.tensor_scalar(out, a, s1, s2, op0, op1)

    def stt(out, in0, scalar, in1, op0, op1):
        return lambda e: e.scalar_tensor_tensor(out, in0, scalar, in1, op0, op1)

    def cp(out, in_):
        return lambda e: e.tensor_copy(out, in_)

    def dmaf(out, in_):
        return lambda e: e.dma_start(out, in_)

    def mm_group(out, pairs):
        def fn(e):
            ins = None
            n = len(pairs)
            for i, (l, r) in enumerate(pairs):
                ins = e.matmul(out, l, r, start=(i == 0), stop=(i == n - 1))
            return ins
        return fn

    def rstd_from(ssq_ap, ssq_buf, nb, inv_n):
        t_ap, t_b = smalls.next()
        r_ap, r_b = smalls.next()
        P.op("act", act(t_ap[:nb], ssq_ap[:nb], AF.Sqrt, bias=EPS, scale=inv_n), [ssq_buf], [t_b])
        P.op("dve", lambda e: e.reciprocal(r_ap[:nb], t_ap[:nb]), [t_b], [r_b])
        return r_ap, r_b

    P.dma("act", dmaf(ident[:], ident_d.ap()), [], [B_const])
    P.dma("act", dmaf(vecs[:], vecs_d.ap()), [], [B_const])
    B_c2 = Buf()
    P.op("pool", lambda e: e.memset(onesm[:], 1.0 / 512.0), [], [B_c2])
    P.op("pool", lambda e: e.memset(onecol[:], 1.0), [], [B_c2])
    P.op("pool", lambda e: e.memset(onesf[:], 1.0), [], [B_c2])
    CONST = [B_const, B_c2]

    mark_persist = A.lo
    ckvnT = sb("ckvnT", [128, 2, NK], BF)
    krotT = sb("krotT", [32, NK], BF)
    cqnT = sb("cqnT", [128, 3, NWIN], BF)
    wuq = sb("wuq", [128, 3, 768], BF)
    wk = sb("wk", [128, 2, 512], BF)
    wv = sb("wv", [128, 2, 512], BF)
    mark_lat = A.lo
    B_lat = Buf()
    B_w3 = Buf()

    def load_cast(dst, src_dram_ap, wbuf):
        P.dma("pool", lambda e: e.dma_start(dst, src_dram_ap, max_dma_last_dim=8192), [], [wbuf])


    if True:
        sb1 = sb
        diag = sb1("diag", [128, 4, 31, 128], BF)
        uT = sb1("uT", [128, 4, UPAD + NWIN], BF)
        mark1 = A.lo
        win = sb1("win", [128, 8, 1696], BF)
        gin = sb1("gin", [128, D])
        gq = sb1("gq", [128, 384])
        gkv = sb1("gkv", [128, 256])
        csk = sb1("csk", [128, 33, 32])
        csq = sb1("csq", [128, 17, 32])
        xts = [sb1("xt%d" % i, [128, D]) for i in range(3)]
        junk = sb1("junk", [128, D], BF)
        nbs = [sb1("nb%d" % i, [128, D], BF) for i in range(2)]
        nTs = [sb1("nT%d" % i, [128, 8, 512], BF) for i in range(3)]
        lat_sb = [sb1("latsb%d" % i, [128, 384], BF) for i in range(4)]
        krot_sb = [sb1("krotsb%d" % i, [128, 32], BF) for i in range(2)]
        rtmp = sb1("rtmp", [128, 4, 16])
        sig_sb = [sb1("sig%d" % i, [128, 512]) for i in range(2)]

        B_win = Buf()
        def win_load(ca, cb_):
            for k0 in range(0, 8, 4):
                load_cast(win[:, k0:k0 + 4, ca:cb_], win_d.ap()[:, k0:k0 + 4, ca:cb_], B_win)
        win_load(1408, 1696)

        def late_loads(g):
            def f():
                if g == 0:
                    win_load(1024, 1408)
                    win_load(0, 512)
                    win_load(512, 1024)
                if g == 2:
                    for kc in range(3):
                        load_cast(wuq[:, kc, :], wuq_d.ap()[:, kc, :], B_w3)
                    for kc in range(2):
                        load_cast(wk[:, kc, :], wk_d.ap()[:, kc, :], B_w3)
                        load_cast(wv[:, kc, :], wv_d.ap()[:, kc, :], B_w3)
            return f
        B_g = Buf()
        P.dma("act", dmaf(gin[:], gin_d.ap()), [], [B_g])
        P.dma("act", dmaf(gq[:], gq_d.ap()), [], [B_g])
        P.dma("act", dmaf(gkv[:], gkv_d.ap()), [], [B_g])
        P.dma("act", dmaf(csk[:], csk_d.ap()), [], [B_g])
        B_uT = [Buf() for _ in range(5)]
        B_upad = Buf()
        P.op("dve", lambda e: e.memset(uT[:, :, 0:UPAD], 0.0), [], [B_upad])
        B_diag = Buf()

        def diag_unit(i):
            def f():
                for k in range(31):
                    c = V_CW + i * 31 + k
                    P.op("dve", ts(diag[:, i, k, :], ident[:], vecs[:, c:c + 1], None, ALU.mult), CONST, [B_diag])
            return f

        xt_r = Rot([(xts[i], Buf()) for i in range(3)])
        nb_r = Rot([(nbs[i], Buf()) for i in range(2)])
        nT_r = Rot([(nTs[i], Buf()) for i in range(3)])
        lat_r = Rot([(lat_sb[i], Buf()) for i in range(4)])
        krot_r = Rot([(krot_sb[i], Buf()) for i in range(2)])
        sig_r = Rot([(sig_sb[i], Buf()) for i in range(2)])
        B_junk = Buf()
        B_rtmp = Buf()
        tp_r = Rot([0, 1])
        tpb_r = Rot([2, 3])
        lt_r = Rot([4, 5])
        ag_r = Rot([(6, 7)])
        pipe1 = Pipe()

        def norm_block(src_rows_ap, nb, gain, nT, nTbuf, r0):
            xt, xb = xt_r.next()
            P.dma("sp", dmaf(xt[:nb], src_rows_ap), [], [xb])
            s_ap, s_b = smalls.next()
            P.op("act", act(junk[:nb], xt[:nb], AF.Square, accum_out=s_ap[:nb]), [xb], [B_junk, s_b])
            r_ap, r_b = rstd_from(s_ap, s_b, nb, 1.0 / D)
            nbt, nbb = nb_r.next()
            P.op("dve", stt(nbt[:nb], xt[:nb], r_ap[:nb], gain[:nb], ALU.mult, ALU.mult), [xb, r_b, B_g], [nbb])
            bank = tp_r.next()

            def trs(e):
                ins = None
                for kc in range(8):
                    ins = e.transpose(psb[:, bank, kc * 128:kc * 128 + nb], nbt[:nb, kc * 128:(kc + 1) * 128], ident[:nb, :nb])
                return ins
            P.op("pe", trs, [nbb] + CONST, [B_ps[bank]])
            src = psb[:, bank, :].rearrange("p (k t) -> p k t", k=8)[:, :, 0:nb]
            P.op("dve", cp(nT[:, :, r0:r0 + nb], src), [B_ps[bank]], [nTbuf])

        def rope_T(src_ps, nb, cs_ap, out_bf, nheads, in_bufs, out_buf):
            cos = bass.AP(cs_ap.tensor, cs_ap.offset, [cs_ap.ap[0], [0, nheads], [1, 16]])
            sin = bass.AP(cs_ap.tensor, cs_ap.offset + 16, [cs_ap.ap[0], [0, nheads], [1, 16]])
            x1 = src_ps[:, :, 0:16]
            x2 = src_ps[:, :, 16:32]
            t = [rtmp[:nb, i, :] for i in range(4)]
            if nheads > 1:
                raise NotImplementedError
            t = [bass.AP(a.tensor, a.offset, [a.ap[0], [0, 1], [1, 16]]) for a in t]
            P.op("dve", tt(t[0], x1, cos, ALU.mult), in_bufs + [B_g], [B_rtmp])
            P.op("dve", tt(t[1], x2, sin, ALU.mult), in_bufs + [B_g], [B_rtmp])
            P.op("dve", tt(out_bf[:, :, 0:16], t[0], t[1], ALU.subtract), [B_rtmp], [out_buf])
            P.op("dve", tt(t[2], x2, cos, ALU.mult), in_bufs + [B_g], [B_rtmp])
            P.op("dve", tt(t[3], x1, sin, ALU.mult), in_bufs + [B_g], [B_rtmp])
            P.op("dve", tt(out_bf[:, :, 16:32], t[2], t[3], ALU.add), [B_rtmp], [out_buf])

        pending_ag = []

        def process_group(src_d, row0, ntok, kcol, wcol, csk_blk0, csq_blk0):
            gst = {}

            def g_alloc():
                gst["nT"], gst["nTb"] = nT_r.next()
            pipe1.unit([g_alloc])
            for bi, (r0, nb) in enumerate(blocks_of(ntok)):
                def mk(bi=bi, r0=r0, nb=nb):
                    st = {}
                    src_rows = src_d.ap()[row0 + r0:row0 + r0 + nb, :]

                    def s0():
                        xt, xb = xt_r.next()
                        st["xt"], st["xb"] = xt, xb
                        P.dma("sp", dmaf(xt[:nb], src_rows), [], [xb])

                    def s1():
                        xt, xb = st["xt"], st["xb"]
                        s_ap, s_b = smalls.next()
                        st["ssq"] = (s_ap, s_b)
                        P.op("act", act(junk[:nb], xt[:nb], AF.Square, accum_out=s_ap[:nb]), [xb], [B_junk, s_b])

                    def s2():
                        xt, xb = st["xt"], st["xb"]
                        s_ap, s_b = st["ssq"]
                        r_ap, r_b = rstd_from(s_ap, s_b, nb, 1.0 / D)
                        nbt, nbb = nb_r.next()
                        st["nbt"], st["nbb"] = nbt, nbb
                        P.op("dve", stt(nbt[:nb], xt[:nb], r_ap[:nb], gin[:nb], ALU.mult, ALU.mult), [xb, r_b, B_g], [nbb])

                    def s3():
                        nbt, nbb = st["nbt"], st["nbb"]
                        bank = tp_r.next()
                        st["tb0"] = bank

                        def trs(e):
                            ins = None
                            for kc in range(8):
                                ins = e.transpose(psb[:, bank, kc * 128:kc * 128 + nb], nbt[:nb, kc * 128:(kc + 1) * 128], ident[:nb, :nb])
                            return ins
                        P.op("pe", trs, [nbb] + CONST, [B_ps[bank]])

                    def s4():
                        bank = st["tb0"]
                        nT, nTbuf = gst["nT"], gst["nTb"]
                        src = psb[:, bank, :].rearrange("p (k t) -> p k t", k=8)[:, :, 0:nb]
                        P.op("act", lambda e: e.activation(nT[:, :, r0:r0 + nb], src, AF.Copy), [B_ps[bank]], [nTbuf])

                    def s5():
                        nT, nTbuf = gst["nT"], gst["nTb"]
                        if kcol is not None:
                            bank = lt_r.next()
                            st["kvb"] = bank
                            P.op("pe", mm_group(ps[:nb, bank, 0:288], [(nT[:, kc, r0:r0 + nb], win[:, kc, 1408:1696]) for kc in range(8)]),
                                 [nTbuf, B_win], [B_ps[bank]])
                        if wcol is not None:
                            bank = lt_r.next()
                            st["qb"] = bank
                            P.op("pe", mm_group(ps[:nb, bank, 0:384], [(nT[:, kc, r0:r0 + nb], win[:, kc, 1024:1408]) for kc in range(8)]),
                                 [nTbuf, B_win], [B_ps[bank]])

                    def s6():
                        if kcol is not None:
                            bank = st["kvb"]
                            s_ap, s_b = smalls.next()
                            P.op("act", act(junk[:nb, 0:256], ps[:nb, bank, 0:256], AF.Square, accum_out=s_ap[:nb]), [B_ps[bank]], [B_junk, s_b])
                            r_ap, r_b = rstd_from(s_ap, s_b, nb, 1.0 / 256)
                            lt, ltb = lat_r.next()
                            st["kvl"] = (lt, ltb)
                            P.op("dve", stt(lt[:nb, 0:256], ps[:nb, bank, 0:256], r_ap[:nb], gkv[:nb], ALU.mult, ALU.mult), [B_ps[bank], r_b, B_g], [ltb])
                            kr, krb = krot_r.next()
                            st["kr"] = (kr, krb)
                            src3 = ps[:nb, bank, 256:288].rearrange("p (h d) -> p h d", h=1)
                            out3 = kr[:nb, :].rearrange("p (h d) -> p h d", h=1)
                            rope_T(src3, nb, csk[:nb, csk_blk0 + bi, :], out3, 1, [B_ps[bank]], krb)
                        if wcol is not None:
                            bank = st["qb"]
                            s_ap, s_b = smalls.next()
                            P.op("act", act(junk[:nb, 0:384], ps[:nb, bank, 0:384], AF.Square, accum_out=s_ap[:nb]), [B_ps[bank]], [B_junk, s_b])
                            r_ap, r_b = rstd_from(s_ap, s_b, nb, 1.0 / 384)
                            lt, ltb = lat_r.next()
                            st["ql"] = (lt, ltb)
                            P.op("dve", stt(lt[:nb, 0:384], ps[:nb, bank, 0:384], r_ap[:nb], gq[:nb], ALU.mult, ALU.mult), [B_ps[bank], r_b, B_g], [ltb])

                    def s8():
                        if kcol is not None:
                            lt, ltb = st["kvl"]
                            kr, krb = st["kr"]
                            tb = tpb_r.next()
                            st["kvt"] = tb

                            def trs(e, lt=lt, kr=kr, tb=tb):
                                e.transpose(psb[:, tb, 0:nb], lt[:nb, 0:128], ident[:nb, :nb])
                                e.transpose(psb[:, tb, 128:128 + nb], lt[:nb, 128:256], ident[:nb, :nb])
                                return e.transpose(psb[0:32, tb, 256:256 + nb], kr[:nb, 0:32], ident[:nb, :nb])
                            P.op("pe", trs, [ltb, krb] + CONST, [B_ps[tb]])
                        if wcol is not None:
                            lt, ltb = st["ql"]
                            tb = tpb_r.next()
                            st["qt"] = tb

                            def trs2(e, lt=lt, tb=tb):
                                ins = None
                                for k in range(3):
                                    ins = e.transpose(psb[:, tb, k * 128:k * 128 + nb], lt[:nb, k * 128:(k + 1) * 128], ident[:nb, :nb])
                                return ins
                            P.op("pe", trs2, [ltb] + CONST, [B_ps[tb]])

                    def s9():
                        if kcol is not None:
                            tb = st["kvt"]
                            c0 = kcol + r0
                            srck = psb[:, tb, 0:256].rearrange("p (k t) -> p k t", k=2)[:, :, 0:nb]
                            P.op("act", lambda e, srck=srck, c0=c0: e.activation(ckvnT[:, :, c0:c0 + nb], srck, AF.Copy), [B_ps[tb]], [B_lat])
                            P.op("act", lambda e, tb=tb, c0=c0: e.activation(krotT[0:32, c0:c0 + nb], psb[0:32, tb, 256:256 + nb], AF.Copy), [B_ps[tb]], [B_lat])
                        if wcol is not None:
                            tb = st["qt"]
                            c1 = wcol + r0
                            srcq = psb[:, tb, 0:384].rearrange("p (k t) -> p k t", k=3)[:, :, 0:nb]
                            P.op("act", lambda e, srcq=srcq, c1=c1: e.activation(cqnT[:, :, c1:c1 + nb], srcq, AF.Copy), [B_ps[tb]], [B_lat])
                    pipe1.unit([s0, None, s1, s2, s3, s4, s5, s6, s8, s9])
                mk()
                if pending_ag:
                    pipe1.unit(pending_ag.pop(0))
            if wcol is not None:
                wt = wtile(wcol)
                for i in range(4):
                    def mk2(i=i):
                        st = {}

                        def a0():
                            nT, nTbuf = gst["nT"], gst["nTb"]
                            ba, bg = ag_r.next()
                            st["b"] = (ba, bg)
                            P.op("pe", mm_group(ps[:, ba, 0:ntok], [(win[:, kc, i * 128:(i + 1) * 128], nT[:, kc, 0:ntok]) for kc in range(8)]),
                                 [nTbuf, B_win], [B_ps[ba]])
                            P.op("pe", mm_group(ps[:, bg, 0:ntok], [(win[:, kc, 512 + i * 128:512 + (i + 1) * 128], nT[:, kc, 0:ntok]) for kc in range(8)]),
                                 [nTbuf, B_win], [B_ps[bg]])

                        def a1():
                            ba, bg = st["b"]
                            sg, sgb = sig_r.next()
                            P.op("act", act(sg[:, 0:ntok], ps[:, bg, 0:ntok], AF.Sigmoid), [B_ps[bg]], [sgb])
                            P.op("dve", tt(uT[:, i, UPAD + wcol:UPAD + wcol + ntok], ps[:, ba, 0:ntok], sg[:, 0:ntok], ALU.mult),
                                 [B_ps[ba], sgb, B_upad], [B_uT[wt]])
                        pending_ag.append([None, None, None, None, a0, a1])
                    mk2()

        def prefix_group(g):
            process_group(xp_d, g * 512, 512, g * 512, None, g * 4, None)
            pipe1.unit([diag_unit(g), late_loads(g)])
        if stop >= 1:
            for g in range(3):
                prefix_group(g)
            process_group(meta_d, 0, NMETA, NPRE, None, 16, None)
            process_group(xw_d, 0, HALO, None, 0, None, 0)
            for g in range(4):
                process_group(xw_d, HALO + g * 512, 512, KOWN + g * 512, HALO + g * 512, 17 + g * 4, 1 + g * 4)
            prefix_group(3)
        while pending_ag:
            pipe1.unit(pending_ag.pop(0))
        pipe1.emit()

        if debug and "ckvnT" in debug:
            P.dma("sp", dmaf(dbg_d["ckvnT"].ap(), ckvnT[:]), [B_lat], [])
        if debug and "krotT" in debug:
            P.dma("sp", dmaf(dbg_d["krotT"].ap(), krotT[:]), [B_lat], [])
        if debug and "cqnT" in debug:
            P.dma("sp", dmaf(dbg_d["cqnT"].ap(), cqnT[:]), [B_lat], [])
        if debug and "uT" in debug:
            P.dma("sp", dmaf(dbg_d["uT"].ap(), uT[:]), B_uT + [B_upad], [])

        P.barrier()
        A.lo = mark1
        ysets = []
        for k in range(2):
            ysets.append(dict(y=sb1("y_sb%d" % k, [128, 4, 512]), ybf=sb1("ybf%d" % k, [128, 4, 512], BF),
                              ysq=sb1("ysq%d" % k, [128, 4, 512], BF), stat=sb1("stat%d" % k, [128, 3, 512]),
                              By=[Buf() for _ in range(4)], Bybf=Buf(), Bysq=Buf(), Bstat=Buf()))
        tmp2 = [sb1("tmp2_%d" % i, [128, 512]) for i in range(2)]
        tmp_r = Rot([(tmp2[i], Buf()) for i in range(2)])
        cv_r = Rot([0, 1, 2, 3])
        stb_r = Rot([(4, 5), (6, 7)])
        pipe2 = Pipe()
        for ti, (c0, n) in enumerate([(0, HALO)] + [(HALO + g * 512, 512) for g in range(4)] if stop >= 2 else []):
            wt = wtile(c0)
            ys = ysets[ti % 2]
            y_sb, ybf, ysq, stat = ys["y"], ys["ybf"], ys["ysq"], ys["stat"]
            B_y, B_ybf, B_ysq, B_stat = ys["By"], ys["Bybf"], ys["Bysq"], ys["Bstat"]
            for i in range(4):
                def mkc(i=i, c0=c0, n=n, wt=wt, y_sb=y_sb, ybf=ybf, ysq=ysq, B_y=B_y, B_ybf=B_ybf, B_ysq=B_ysq):
                    st = {}

                    def c0_():
                        bank = cv_r.next()
                        st["b"] = bank
                        P.op("pe", mm_group(ps[:, bank, 0:n], [(diag[:, i, k, :], uT[:, i, c0 + k:c0 + k + n]) for k in range(31)]),
                             [B_diag, B_upad] + B_uT[max(0, wt - 1):wt + 1], [B_ps[bank]])

                    def c1_():
                        bank = st["b"]
                        cb = vecs[:, V_CB + i:V_CB + i + 1]
                        P.op("act", act(y_sb[:, i, 0:n], ps[:, bank, 0:n], AF.Identity, bias=cb), [B_ps[bank]] + CONST, [B_y[i]])
                        P.op("dve", cp(ybf[:, i, 0:n], y_sb[:, i, 0:n]), [B_y[i]], [B_ybf])
                        P.op("act", act(ysq[:, i, 0:n], y_sb[:, i, 0:n], AF.Square), [B_y[i]], [B_ysq])
                    pipe2.unit([c0_, c1_])
                mkc()

            def mks(n=n, ybf=ybf, ysq=ysq, stat=stat, B_ybf=B_ybf, B_ysq=B_ysq, B_stat=B_stat):
                mean = stat[:, 0, 0:n]
                m2 = stat[:, 1, 0:n]
                rs = stat[:, 2, 0:n]
                sst = {}

                def t0_():
                    bm, bq = stb_r.next()
                    sst["b"] = (bm, bq)
                    P.op("pe", mm_group(ps[:, bm, 0:n], [(onesm[:], ybf[:, i, 0:n]) for i in range(4)]), [B_ybf, B_c2], [B_ps[bm]])
                    P.op("pe", mm_group(ps[:, bq, 0:n], [(onesm[:], ysq[:, i, 0:n]) for i in range(4)]), [B_ysq, B_c2], [B_ps[bq]])

                def t1_():
                    bm, bq = sst["b"]
                    P.op("act", act(mean, ps[:, bm, 0:n], AF.Copy), [B_ps[bm]], [B_stat])
                    P.op("act", act(m2, ps[:, bm, 0:n], AF.Square), [B_ps[bm]], [B_stat])
                    P.op("act", act(rs, ps[:, bq, 0:n], AF.Copy), [B_ps[bq]], [B_stat])

                def t2_():
                    P.op("dve", tt(m2, rs, m2, ALU.subtract), [B_stat], [B_stat])
                    P.op("dve", ts(m2, m2, EPS, None, ALU.add), [B_stat], [B_stat])

                def t3_():
                    P.op("act", act(m2, m2, AF.Sqrt), [B_stat], [B_stat])

                def t4_():
                    P.op("dve", lambda e: e.reciprocal(rs, m2), [B_stat], [B_stat])
                pipe2.unit([None, t0_, t1_, t2_, t3_, t4_])
            mks()
            for i in range(4):
                def mka(i=i, c0=c0, n=n, wt=wt, y_sb=y_sb, stat=stat, B_y=B_y, B_stat=B_stat):
                    st = {}
                    mean = stat[:, 0, 0:n]
                    rs = stat[:, 2, 0:n]

                    def p0_():
                        t, tb_ = tmp_r.next()
                        st["t"] = (t, tb_)
                        P.op("dve", tt(t[:, 0:n], y_sb[:, i, 0:n], mean, ALU.subtract), [B_y[i], B_stat], [tb_])
                        P.op("dve", tt(t[:, 0:n], t[:, 0:n], rs, ALU.mult), [tb_, B_stat], [tb_])

                    def p1_():
                        t, tb_ = st["t"]
                        lg = vecs[:, V_LG + i:V_LG + i + 1]
                        lb = vecs[:, V_LB + i:V_LB + i + 1]
                        P.op("act", act(swT[:, i, c0:c0 + n], t[:, 0:n], AF.Silu, bias=lb, scale=lg), [tb_] + CONST, [B_swT[wt]])
                    pipe2.unit([None] * 5 + [p0_, p1_])
                mka()
        pipe2.emit()
        if debug and "swT" in debug:
            P.dma("sp", dmaf(dbg_d["swT"].ap(), swT[:]), B_swT, [])
        P.barrier()
        A.lo = mark_lat

    wout = A.down("wout", [128, 8, D], BF)
    oT = A.down("oT", [128, 4, NWIN], BF)
    B_wout = Buf()
    if True:
        sb3 = sb
        QT = sb3("QT", [128, 8, NWIN], BF)
        Vt = sb3("Vt", [128, 33, 8, 65], BF)
        KTs = [sb3("KT%d" % i, [128, NK], BF) for i in range(2)]
        PTs = [sb3("PT%d" % i, [128, 3, 512], BF) for i in range(2)]
        osbs = [sb3("osb%d" % i, [65, 512]) for i in range(2)]
        rden = sb3("rden", [65, 512])
        qbf = [sb3("qbf%d" % i, [128, 768], BF) for i in range(2)]
        csq3 = sb3("csq3", [128, 17, 128])
        rt3 = sb3("rt3", [128, 4, 4, 16])
        wstage = [sb3("wst%d" % i, [128, 512]) for i in range(2)]
        B_rden0 = Buf()
        P.op("pool", lambda e: e.memset(rden[:], 0.0), [], [B_rden0])

        B_csq3 = Buf()
        P.dma("sp", dmaf(csq3[:], csq4_d.ap()), [], [B_csq3])
        wst_r = Rot([(wstage[i], Buf()) for i in range(2)])
        for kc in range(8):
            for hf in range(2):
                st, stb = wst_r.next()
                P.dma("sp", dmaf(st[:], wout_d.ap()[:, kc, hf * 512:(hf + 1) * 512]), [], [stb])
                P.op("dve", ts(wout[:, kc, hf * 512:(hf + 1) * 512], st[:], vecs[:, V_GOUT + kc:V_GOUT + kc + 1], None, ALU.mult), [stb] + CONST, [B_wout])

        B_QT = Buf()
        B_QTm = Buf()
        B_V = Buf()
        B_Vone = Buf()
        B_KT = [Buf(), Buf()]
        B_KTc = [Buf(), Buf()]
        for h in range(8):
            P.dma("sp", dmaf(QT[96:105, h, :], maskq_d.ap()), [], [B_QTm])
        for i in range(2):
            P.dma("sp", dmaf(KTs[i][96:105, :], maskk_d.ap()), [], [B_KTc[i]])
            P.dma("sp", dmaf(KTs[i][64:96, :], krotT[:, :]), [B_lat], [B_KTc[i]])
        P.op("pool", lambda e: e.memset(Vt[:, :, :, 64:65], 1.0), [], [B_Vone])

        key_blocks = [(j * 128, 128) for j in range(16)] + [(NPRE, NMETA)] + [(KOWN + j * 128, 128) for j in range(16)]
        mb_r = Rot([6, 7])
        pipe3 = Pipe()
        for bi, (c0, nk) in enumerate(key_blocks if stop >= 2.5 else []):
            def mkv(bi=bi, c0=c0, nk=nk):
                st = {}

                def v0():
                    bank = mb_r.next()
                    st["b"] = bank
                    P.op("pe", mm_group(ps[:nk, bank, :], [(ckvnT[:, kc, c0:c0 + nk], wv[:, kc, :]) for kc in range(2)]), [B_lat, B_w3], [B_ps[bank]])

                def v1():
                    bank = st["b"]
                    src = ps[:nk, bank, :].rearrange("p (h d) -> p h d", h=8)
                    P.op("act", lambda e: e.activation(Vt[:nk, bi, :, 0:64], src, AF.Copy), [B_ps[bank], B_Vone], [B_V])
                pipe3.unit([v0, v1])
            mkv()
        win_blocks = [(0, HALO)] + [(HALO + j * 128, 128) for j in range(16)]
        qb_r = Rot([(qbf[i], Buf()) for i in range(2)])
        B_rt3 = Buf()
        q_r = Rot([(0, 1), (2, 3)])
        tq_r = Rot([4, 5])
        for bi, (c0, nb) in enumerate(win_blocks if stop >= 2.9 else []):
            def mkq(bi=bi, c0=c0, nb=nb):
                st = {}

                def q0():
                    b0, b1 = q_r.next()
                    st["bk"] = (b0, b1)
                    for half, bank in ((0, b0), (1, b1)):
                        P.op("pe", mm_group(ps[:nb, bank, 0:384], [(cqnT[:, kc, c0:c0 + nb], wuq[:, kc, half * 384:(half + 1) * 384]) for kc in range(3)]),
                             [B_lat, B_w3], [B_ps[bank]])

                def q1():
                    b0, b1 = st["bk"]
                    qb, qbb = qb_r.next()
                    st["qb"] = (qb, qbb)
                    for half, bank in ((0, b0), (1, b1)):
                        src = ps[:nb, bank, 0:384].rearrange("p (h d) -> p h d", h=4)
                        dst = qb[:nb, half * 384:(half + 1) * 384].rearrange("p (h d) -> p h d", h=4)
                        P.op("act", lambda e, src=src, dst=dst: e.activation(dst[:, :, 0:64], src[:, :, 0:64], AF.Copy), [B_ps[bank]], [qbb])
                        cos = csq3[:nb, bi, 0:64].rearrange("p (h d) -> p h d", h=4)
                        sin = csq3[:nb, bi, 64:128].rearrange("p (h d) -> p h d", h=4)
                        x1 = src[:, :, 64:80]
                        x2 = src[:, :, 80:96]
                        t = [rt3[:nb, i, 0:4, :] for i in range(4)]
                        P.op("dve", tt(t[0], x1, cos, ALU.mult), [B_ps[bank], B_csq3], [B_rt3])
                        P.op("dve", tt(t[1], x2, sin, ALU.mult), [B_ps[bank], B_csq3], [B_rt3])
                        P.op("dve", tt(dst[:, :, 64:80], t[0], t[1], ALU.subtract), [B_rt3], [qbb])
                        P.op("dve", tt(t[2], x2, cos, ALU.mult), [B_ps[bank], B_csq3], [B_rt3])
                        P.op("dve", tt(t[3], x1, sin, ALU.mult), [B_ps[bank], B_csq3], [B_rt3])
                        P.op("dve", tt(dst[:, :, 80:96], t[2], t[3], ALU.add), [B_rt3], [qbb])

                def q2():
                    qb, qbb = st["qb"]
                    tb = tq_r.next()
                    st["tb"] = tb

                    def trs(e):
                        ins = None
                        for h in range(8):
                            ins = e.transpose(psb[0:96, tb, h * 128:h * 128 + nb], qb[:nb, h * 96:(h + 1) * 96], ident[:nb, :nb])
                        return ins
                    P.op("pe", trs, [qbb] + CONST, [B_ps[tb]])

                def q3():
                    tb = st["tb"]
                    src = psb[0:96, tb, :].rearrange("p (h t) -> p h t", h=8)[:, :, 0:nb]
                    P.op("act", lambda e: e.activation(QT[0:96, :, c0:c0 + nb], src, AF.Copy), [B_ps[tb]], [B_QT])
                pipe3.unit([q0, q1, q2, q3])
            mkq()
        pipe3.emit()
        if debug and "QT" in debug:
            P.dma("sp", dmaf(dbg_d["QT"].ap(), QT[0:105, :, :]), [B_QT, B_QTm], [])
        if debug and "Vt" in debug:
            P.dma("sp", dmaf(dbg_d["Vt"].ap(), Vt[:]), [B_V, B_Vone], [])

        key_tiles = [(g * 512, 512) for g in range(4)] + [(NPRE, NMETA)] + [(KOWN + g * 512, 512) for g in range(4)]
        q_tiles = [(0, HALO, -1)] + [(HALO + g * 512, 512, g) for g in range(4)]
        S_r = Rot([(0, 1), (2, 3), (4, 5)])
        PT_r = Rot([(PTs[i], Buf()) for i in range(2)])
        osb_r = Rot([(osbs[i], Buf()) for i in range(2)])
        B_rden = Buf()
        OB = 6
        BB = 7
        pipe = Pipe()

        k_units = []

        def add_K(h):
            KT = KTs[h % 2]
            BK = B_KT[h % 2]
            for (c0, n) in key_tiles:
                stk = {}

                def s_mm(h=h, c0=c0, n=n, stk=stk):
                    bank = S_r.next()[0]
                    stk["bank"] = bank
                    P.op("pe", mm_group(ps[0:64, bank, 0:n], [(wk[:, kc, h * 64:(h + 1) * 64], ckvnT[:, kc, c0:c0 + n]) for kc in range(2)]),
                         [B_lat, B_w3], [B_ps[bank]])

                def s_cp(KT=KT, BK=BK, c0=c0, n=n, stk=stk):
                    bank = stk["bank"]
                    P.op("dve", cp(KT[0:64, c0:c0 + n], ps[0:64, bank, 0:n]), [B_ps[bank]], [BK])
                k_units.append([s_mm, s_cp])

        def add_group(h, qc, nq, grp, first, last):
            KT = KTs[h % 2]
            BK = B_KT[h % 2]
            st = {}
            gs = len(grp)
            nk = grp[0][1]
            qo = grp[0][4]

            def s_qk():
                banks = S_r.next()
                st["banks"] = banks

                def qk(e):
                    ins = None
                    for (kc0, nk_, K, bi, qo_), bank in zip(grp, banks):
                        ins = e.matmul(ps[:nk_, bank, qo_:nq], KT[0:K, kc0:kc0 + nk_], QT[0:K, h, qc + qo_:qc + nq], start=True, stop=True)
                    return ins
                P.op("pe", qk, [BK, B_KTc[h % 2], B_QT, B_QTm], [B_ps[b] for b in banks[:gs]])

            def s_exp():
                banks = st["banks"]
                pt, ptb = PT_r.next()
                st["pt"], st["ptb"] = pt, ptb
                b0 = banks[0]
                P.op("act", lambda e: e.activation(pt[:nk, 0:gs, qo:nq], ps[:nk, b0:b0 + gs, qo:nq], AF.Exp, scale=SCALE),
                     [B_ps[b] for b in banks[:gs]], [ptb])

            def s_pv():
                pt, ptb = st["pt"], st["ptb"]

                def pv(e):
                    ins = None
                    for i, (kc0, nk_, K, bi, qo_) in enumerate(grp):
                        ins = e.matmul(ps[0:65, OB, qo_:nq], Vt[:nk_, bi, h, :], pt[:nk_, i, qo_:nq],
                                       start=(first and i == 0), stop=(last and i == gs - 1))
                    return ins
                P.op("pe", pv, [ptb, B_V, B_Vone], [B_ps[OB]])
            pipe.unit([s_qk, None, s_exp, s_pv])

        def add_epilogue(h, qc, nq):
            st = {}

            def s_cp():
                osb, osbb = osb_r.next()
                st["osb"], st["osbb"] = osb, osbb
                P.op("dve", cp(osb[:, 0:nq], ps[0:65, OB, 0:nq]), [B_ps[OB]], [osbb])

            def s_rc():
                osb, osbb = st["osb"], st["osbb"]
                P.op("dve", lambda e: e.reciprocal(rden[64:65, 0:nq], osb[64:65, 0:nq]), [osbb, B_rden0], [B_rden])

            def s_bc():
                P.op("pe", mm_group(ps[0:64, BB, 0:nq], [(onesf[0:65, 0:64], rden[0:65, 0:nq])]), [B_rden, B_rden0, B_c2], [B_ps[BB]])

            def s_mul():
                osb, osbb = st["osb"], st["osbb"]
                po = (h % 2) * 64
                P.op("dve", tt(oT[po:po + 64, h // 2, qc:qc + nq], osb[0:64, 0:nq], ps[0:64, BB, 0:nq], ALU.mult),
                     [osbb, B_ps[BB]], [B_oT[wtile(qc)]])
            pipe.unit([None, None, None, None, s_cp, s_rc, None, None, None, s_bc, s_mul])

        if stop >= 4:
            add_K(0)
            while k_units:
                pipe.unit(k_units.pop(0))
        for h in range(8 if stop >= 4 else 0):
            for qi, (qc, nq, g) in enumerate(q_tiles):
                blks = [(j * 128, 128, 97, j, 0) for j in range(16)] + [(NPRE, NMETA, 96, 16, 0)]
                if g >= 0:
                    for kb in range(4 * g + 4):
                        dj = kb - 4 * g
                        blks.append((KOWN + kb * 128, 128, 105 if dj >= 0 else 96, 17 + kb, 128 * dj if dj > 0 else 0))
                groups = []
                cur = []
                for b in blks:
                    if cur and (len(cur) == 2 or cur[0][1] != b[1] or cur[0][4] != b[4]):
                        groups.append(cur)
                        cur = []
                    cur.append(b)
                if cur:
                    groups.append(cur)
                for gi, grp in enumerate(groups):
                    add_group(h, qc, nq, grp, gi == 0, gi == len(groups) - 1)
                    if k_units and gi % 2 == 1:
                        pipe.unit(k_units.pop(0))
                add_epilogue(h, qc, nq)
                if qi == 0 and h + 1 < 8:
                    add_K(h + 1)
                if qi == len(q_tiles) - 1:
                    while k_units:
                        pipe.unit(k_units.pop(0))
        pipe.emit()
        if debug and "oT" in debug:
            P.dma("sp", dmaf(dbg_d["oT"].ap(), oT[:]), B_oT, [])
        P.barrier()
    A.lo = mark_persist

    if True:
        sb5 = sb
        gfin = sb5("gfin", [128, D])
        gffn = sb5("gffn", [128, D])
        P.dma("sp", dmaf(gfin[:], gfin_d.ap()), [], [B_const])
        P.dma("sp", dmaf(gffn[:], gffn_d.ap()), [], [B_const])
        wdn = sb5("wdn", [128, 22, D], BF)
        h1 = sb5("h1", [128, 6, D])
        nT2 = sb5("nT2", [128, 8, 688], BF)
        actT = sb5("actT", [128, 22, 688], BF)
        wups = [sb5("wup%d" % i, [128, 8, 256], BF) for i in range(2)]
        xts = [sb5("xt5_%d" % i, [128, D]) for i in range(2)]
        sqs = [sb5("sq5_%d" % i, [128, 4, 128], BF) for i in range(2)]
        junk = sb5("junk5", [128, D], BF)
        n2s = [sb5("n2_%d" % i, [128, D], BF) for i in range(1)]
        accs = [sb5("acc%d" % i, [128, 512]) for i in range(4)]
        sgs = [sb5("sg%d" % i, [128, 512]) for i in range(2)]
        outs = [sb5("outs%d" % i, [128, D]) for i in range(1)]

        B_wdn = Buf()
        wdn_loaded = [False]
        B_actz = Buf()
        P.op("pool", lambda e: e.memset(actT[:, :, 0:2], 0.0), [], [B_actz])
        P.op("pool", lambda e: e.memset(actT[:, :, 512:514], 0.0), [], [B_actz])
        xt_r = Rot([(xts[i], Buf()) for i in range(2)])
        sq_r = Rot([(sqs[i], Buf()) for i in range(2)])
        n2_r = Rot([(n2s[i], Buf()) for i in range(1)])
        wup_r = Rot([(wups[i], Buf()) for i in range(2)])
        acc_r = Rot([(accs[i], Buf()) for i in range(4)])
        sg_r = Rot([(sgs[i], Buf()) for i in range(2)])
        out_r = Rot([(outs[i], Buf()) for i in range(1)])
        B_junk = Buf()
        B_h1 = [Buf() for _ in range(6)]
        B_nT2 = Buf()
        B_actT = Buf()
        st_r = Rot([6, 7])
        up_r = Rot([(0, 1), (2, 3)])
        dn_r = Rot([(4, 5)])
        tp_r = Rot([4, 5])

        up3_r = Rot([(0, 1), (2, 3), (4, 5)])
        pipe5 = None
        for macro in (FFN_MACROS if stop >= 5 else []):
            offs = []
            o = 0
            for (s, n) in macro:
                offs.append(o)
                o += n + 2
            blist = []
            for si, (s, n) in enumerate(macro):
                for (r0, nb) in blocks_of(n + 2):
                    blist.append((si, s - 2 + r0, nb, offs[si] + r0, len(blist)))
            if pipe5 is None:
                pipe5 = Pipe()
            else:
                pipe5.unit([None])
                pipe5.unit([None])
            pipe = pipe5
            for (si, tok0, nb, col, hs) in blist:
                def mk(si=si, tok0=tok0, nb=nb, col=col, hs=hs):
                    st = {}
                    srcs = [B_swT[t] for t in wtiles(tok0, nb)] + [B_oT[t] for t in wtiles(tok0, nb)]

                    def s0():
                        st["sq"] = []
                        for (srcT, Bsrc) in ((swT, B_swT), (oT, B_oT)):
                            sq, sqb = sq_r.next()
                            P.op("pool", tt(sq[:, :, 0:nb], srcT[:, :, tok0:tok0 + nb], srcT[:, :, tok0:tok0 + nb], ALU.mult),
                                 [Bsrc[t] for t in wtiles(tok0, nb)], [sqb])
                            st["sq"].append((sq, sqb))

                    def s1():
                        st["bk"] = []
                        for (sq, sqb) in st["sq"]:
                            bank = st_r.next()
                            P.op("pe", mm_group(ps[:nb, bank, 0:1], [(sq[:, i, 0:nb], onecol[:, 0:1]) for i in range(4)]), [sqb, B_c2], [B_ps[bank]])
                            st["bk"].append(bank)

                    def s2():
                        xt, xb = xt_r.next()
                        st["xt"], st["xb"] = xt, xb
                        P.dma("sp", dmaf(xt[:nb], xw_d.ap()[tok0:tok0 + nb, :]), [], [xb])
                        st["rst"] = []
                        for bank in st["bk"]:
                            s_ap, s_b = smalls.next()
                            P.op("act", act(s_ap[:nb], ps[:nb, bank, 0:1], AF.Copy), [B_ps[bank]], [s_b])
                            st["rst"].append(rstd_from(s_ap, s_b, nb, 1.0 / 512))

                    def s3():
                        ba = up_r.next()
                        bb = up_r.next()
                        st["ba"], st["bb"] = ba, bb
                        for half in range(2):
                            P.op("pe", mm_group(ps[:nb, ba[half], :], [(swT[:, i, tok0:tok0 + nb], wout[:, i, half * 512:(half + 1) * 512]) for i in range(4)]),
                                 srcs + [B_wout], [B_ps[ba[half]]])
                            P.op("pe", mm_group(ps[:nb, bb[half], :], [(oT[:, i, tok0:tok0 + nb], wout[:, 4 + i, half * 512:(half + 1) * 512]) for i in range(4)]),
                                 srcs + [B_wout], [B_ps[bb[half]]])

                    def s4():
                        ba, bb, xt, xb, rst = st["ba"], st["bb"], st["xt"], st["xb"], st["rst"]
                        for half in range(2):
                            hv = h1[:nb, hs, half * 512:(half + 1) * 512]
                            P.op("dve", stt(hv, ps[:nb, ba[half], :], rst[0][0][:nb], xt[:nb, half * 512:(half + 1) * 512], ALU.mult, ALU.add),
                                 [B_ps[ba[half]], rst[0][1], xb], [B_h1[hs]])
                            P.op("dve", stt(hv, ps[:nb, bb[half], :], rst[1][0][:nb], hv, ALU.mult, ALU.add),
                                 [B_ps[bb[half]], rst[1][1], B_h1[hs]], [B_h1[hs]])

                    def s5():
                        s_ap, s_b = smalls.next()
                        P.op("act", act(junk[:nb], h1[:nb, hs, :], AF.Square, accum_out=s_ap[:nb]), [B_h1[hs]], [B_junk, s_b])
                        r_ap, r_b = rstd_from(s_ap, s_b, nb, 1.0 / D)
                        n2, n2b = n2_r.next()
                        st["n2"], st["n2b"] = n2, n2b
                        P.op("dve", stt(n2[:nb], h1[:nb, hs, :], r_ap[:nb], gffn[:nb], ALU.mult, ALU.mult), [B_h1[hs], r_b] + CONST, [n2b])

                    def s6():
                        n2, n2b = st["n2"], st["n2b"]
                        tb = tp_r.next()
                        st["tb"] = tb

                        def trs(e):
                            ins = None
                            for kc in range(8):
                                ins = e.transpose(psb[:, tb, kc * 128:kc * 128 + nb], n2[:nb, kc * 128:(kc + 1) * 128], ident[:nb, :nb])
                            return ins
                        P.op("pe", trs, [n2b] + CONST, [B_ps[tb]])

                    def s7():
                        tb = st["tb"]
                        src = psb[:, tb, :].rearrange("p (k t) -> p k t", k=8)[:, :, 0:nb]
                        P.op("act", lambda e: e.activation(nT2[:, :, col:col + nb], src, AF.Copy), [B_ps[tb]], [B_nT2])
                    pipe.unit([s0, s1, s2, s3, s4, s5, s6, s7])
                mk()
            for j in range(22):
                wst = {}
                for si, (s, n) in enumerate(macro):
                    def mk(j=j, si=si, s=s, n=n, wst=wst):
                        st = {}
                        c0 = offs[si]

                        def s_dma():
                            wu, wub = wup_r.next()
                            wst["wu"], wst["wub"] = wu, wub
                            P.dma("pool", lambda e: e.dma_start(wu[:], wup_d.ap()[j], max_dma_last_dim=8192), [], [wub])
                            if not wdn_loaded[0]:
                                load_cast(wdn[:, j, :], wdn_d.ap()[:, j, :], B_wdn)
                                if j == 21:
                                    wdn_loaded[0] = True

                        def s_mm():
                            wu, wub = wst["wu"], wst["wub"]
                            bg, bv = up3_r.next()
                            st["bg"], st["bv"] = bg, bv
                            P.op("pe", mm_group(ps[:, bg, 0:n + 2], [(wu[:, kc, 0:128], nT2[:, kc, c0:c0 + n + 2]) for kc in range(8)]), [wub, B_nT2], [B_ps[bg]])
                            P.op("pe", mm_group(ps[:, bv, 0:n + 2], [(wu[:, kc, 128:256], nT2[:, kc, c0:c0 + n + 2]) for kc in range(8)]), [wub, B_nT2], [B_ps[bv]])

                        def taps(cc):
                            w0 = vecs[:, V_FW + cc * 3 + 0:V_FW + cc * 3 + 1]
                            w1 = vecs[:, V_FW + cc * 3 + 1:V_FW + cc * 3 + 2]
                            w2 = vecs[:, V_FW + cc * 3 + 2:V_FW + cc * 3 + 3]
                            fb = vecs[:, V_FB + cc:V_FB + cc + 1]
                            return w0, w1, w2, fb

                        def s_act():
                            st["res"] = []
                            for (bank, cc) in ((st["bg"], j), (st["bv"], 22 + j)):
                                ac, acb = acc_r.next()
                                w0, w1, w2, fb = taps(cc)
                                P.op("act", act(ac[:, 0:n], ps[:, bank, 2:n + 2], AF.Identity, bias=fb, scale=w2), [B_ps[bank]] + CONST, [acb])
                                st["res"].append((ac, acb))

                        def s_dve():
                            for (bank, cc), (ac, acb) in zip(((st["bg"], j), (st["bv"], 22 + j)), st["res"]):
                                w0, w1, w2, fb = taps(cc)
                                P.op("dve", stt(ac[:, 0:n], ps[:, bank, 1:n + 1], w1, ac[:, 0:n], ALU.mult, ALU.add), [B_ps[bank], acb] + CONST, [acb])
                                P.op("dve", stt(ac[:, 0:n], ps[:, bank, 0:n], w0, ac[:, 0:n], ALU.mult, ALU.add), [B_ps[bank], acb] + CONST, [acb])

                        def s_gate():
                            res = st["res"]
                            sg, sgb = sg_r.next()
                            P.op("act", act(sg[:, 0:n], res[0][0][:, 0:n], AF.Silu), [res[0][1]], [sgb])
                            P.op("pool", tt(actT[:, j, c0 + 2:c0 + 2 + n], sg[:, 0:n], res[1][0][:, 0:n], ALU.mult), [sgb, res[1][1], B_actz], [B_actT])
                        pipe.unit([None, None, None, None, s_dma if si == 0 else None, None, s_mm, s_act, s_dve, s_gate])
                    mk()
            for (si, tok0, nb, col, hs) in blist:
                def mk(si=si, tok0=tok0, nb=nb, col=col, hs=hs):
                    st = {}
                    s, n = macro[si]

                    def d0():
                        b2 = dn_r.next()
                        st["b2"] = b2
                        for half in range(2):
                            P.op("pe", mm_group(ps[:nb, b2[half], :], [(actT[:, j, col:col + nb], wdn[:, j, half * 512:(half + 1) * 512]) for j in range(22)]),
                                 [B_actT, B_actz, B_wdn], [B_ps[b2[half]]])

                    def d1():
                        b2 = st["b2"]
                        for half in range(2):
                            hv = h1[:nb, hs, half * 512:(half + 1) * 512]
                            P.op("dve", tt(hv, ps[:nb, b2[half], :], hv, ALU.add), [B_ps[b2[half]], B_h1[hs]], [B_h1[hs]])

                    def d2():
                        s_ap, s_b = smalls.next()
                        P.op("act", act(junk[:nb], h1[:nb, hs, :], AF.Square, accum_out=s_ap[:nb]), [B_h1[hs]], [B_junk, s_b])
                        st["r"] = rstd_from(s_ap, s_b, nb, 1.0 / D)

                    def d3():
                        r_ap, r_b = st["r"]
                        ot, otb = out_r.next()
                        st["ot"], st["otb"] = ot, otb
                        P.op("dve", stt(ot[:nb], h1[:nb, hs, :], r_ap[:nb], gfin[:nb], ALU.mult, ALU.mult), [B_h1[hs], r_b] + CONST, [otb])

                    def d4():
                        ot, otb = st["ot"], st["otb"]
                        lo = max(tok0, s)
                        hi = tok0 + nb
                        if hi > lo:
                            P.dma("sp", dmaf(out_d.ap()[lo - HALO:hi - HALO, :], ot[lo - tok0:nb, :]), [otb], [])
                    pipe.unit([None] * 8 + [d0, d1, d2, d3, d4])
                mk()
        if pipe5 is not None:
            pipe5.emit()
        P.barrier()

    sems = {}
    for k in sorted(P.semkeys, key=str):
        sems[k] = es.enter_context(nc.semaphore("s_" + "_".join(str(x) for x in (k if isinstance(k, tuple) else (k,)))))
    engobj = {"pe": "tensor", "act": "scalar", "dve": "vector", "pool": "gpsimd", "sp": "sync"}
    with nc.Block() as block:
        for ename in Prog.ENG:
            def body(e, ename=ename):
                for (waits, fn, inc) in P.q[ename]:
                    for (k, v) in waits:
                        e.wait_ge(sems[k], v)
                    if fn is None:
                        continue
                    ins = fn(e)
                    ins.then_inc(sems[inc[0]], inc[1])
            getattr(block, engobj[ename])(body)
    es.close()
    return nc


def _rows(v, n):
    return np.ascontiguousarray(np.asarray(v, np.float32).reshape(n, 128).T)


def _prep_shared(inp):
    f = lambda a: np.asarray(a, np.float32)
    sh = {}
    sh["meta"] = np.ascontiguousarray(f(inp["meta_tokens"]))
    sh["ident"] = np.eye(128, dtype=np.float32).astype(ml_dtypes.bfloat16)
    vecs = np.zeros((128, NV), np.float32)
    gout = np.concatenate([f(inp["conv_out_g"])[0], f(inp["attn_out_g"])[0]])
    vecs[:, V_GOUT:V_GOUT + 8] = _rows(gout, 8)
    cw = f(inp["conv_w"])[0]
    vecs[:, V_CW:V_CW + 124] = cw.reshape(31, 4, 128).transpose(2, 1, 0).reshape(128, 124)
    vecs[:, V_CB:V_CB + 4] = _rows(f(inp["conv_b"])[0], 4)
    vecs[:, V_LG:V_LG + 4] = _rows(f(inp["conv_ln_g"])[0], 4)
    vecs[:, V_LB:V_LB + 4] = _rows(f(inp["conv_ln_b"])[0], 4)
    fw = f(inp["ffn_conv_w"])[0]
    vecs[:, V_FW:V_FW + 132] = fw.reshape(3, 44, 128).transpose(2, 1, 0).reshape(128, 132)
    vecs[:, V_FB:V_FB + 44] = _rows(f(inp["ffn_conv_b"])[0], 44)
    sh["vecs"] = vecs
    bc = lambda v: np.ascontiguousarray(np.broadcast_to(f(v).reshape(1, -1), (128, f(v).size)))
    sh["gin"] = bc(inp["mix_norm_g"][0])
    sh["gq"] = bc(inp["q_norm_g"][0])
    sh["gkv"] = bc(inp["kv_norm_g"][0])
    sh["gffn"] = bc(inp["ffn_norm_g"][0])
    sh["gfin"] = bc(inp["final_norm_g"])
    sh["win"] = np.ascontiguousarray(f(inp["w_in"])[0].reshape(8, 128, 1696).transpose(1, 0, 2))
    sh["wuq"] = np.ascontiguousarray(f(inp["w_uq"])[0].reshape(3, 128, 768).transpose(1, 0, 2))
    wukv = f(inp["w_ukv"])[0].reshape(2, 128, 8, 2, 64)
    sh["wk"] = np.ascontiguousarray(wukv[:, :, :, 0, :].transpose(1, 0, 2, 3).reshape(128, 2, 512))
    sh["wv"] = np.ascontiguousarray(wukv[:, :, :, 1, :].transpose(1, 0, 2, 3).reshape(128, 2, 512))
    sh["wout"] = np.ascontiguousarray(f(inp["w_out"])[0].reshape(8, 128, 1024).transpose(1, 0, 2))
    wup = f(inp["w_ffn_up"])[0].reshape(8, 128, 2, 22, 128)
    sh["wup"] = np.ascontiguousarray(wup.transpose(3, 1, 0, 2, 4).reshape(22, 128, 8, 256))
    sh["wdn"] = np.ascontiguousarray(f(inp["w_ffn_down"])[0].reshape(22, 128, 1024).transpose(1, 0, 2))
    mk = np.zeros((9, NK), np.float32)
    mk[0, :NPRE] = 1.0
    ko = np.arange(NOWN)
    lk = (ko % 512) // 64
    for j in range(8):
        mk[1 + j, KOWN + ko[lk == j]] = 1.0
    sh["maskk"] = mk.astype(ml_dtypes.bfloat16)
    return sh


def _cs_table(pos, nblk_rows):
    inv = 1.0 / (10000.0 ** (np.arange(0, 32, 2, dtype=np.float32) / np.float32(32)))
    inv = inv.astype(np.float32)
    out = np.zeros((128, len(pos), 32), np.float32)
    for b, p in enumerate(pos):
        ang = (np.asarray(p, np.float32)[:, None] * inv[None, :]).astype(np.float32)
        out[:len(p), b, 0:16] = np.cos(ang)
        out[:len(p), b, 16:32] = np.sin(ang)
    return out


def _prep_core(inp, b, s):
    x = np.asarray(inp["x"], np.float32)[b]
    meta = np.asarray(inp["meta_tokens"], np.float32)
    m = {}
    xw = np.zeros((NWIN, D), np.float32)
    if s == 0:
        xw[16:32] = meta
        xw[32:] = x[0:2048]
        xp = np.zeros((NPRE, D), np.float32)
        pos_pre = np.zeros(NPRE)
        pos_halo = np.concatenate([np.zeros(16), np.arange(16)])
        pos_own = 16 + np.arange(2048)
    else:
        xw[:] = x[2016:4096]
        xp = np.ascontiguousarray(x[0:2048])
        pos_pre = 16 + np.arange(2048)
        pos_halo = 16 + 2016 + np.arange(32)
        pos_own = 16 + 2048 + np.arange(2048)
    m["xw"] = xw
    m["xp"] = xp
    kpos = [pos_pre[i * 128:(i + 1) * 128] for i in range(16)] + [np.arange(16)] + [pos_own[i * 128:(i + 1) * 128] for i in range(16)]
    m["csk"] = _cs_table(kpos, 33)
    qpos = [pos_halo] + [pos_own[i * 128:(i + 1) * 128] for i in range(16)]
    m["csq"] = _cs_table(qpos, 17)
    c4 = np.zeros((128, 17, 128), np.float32)
    c4[:, :, 0:64] = np.tile(m["csq"][:, :, 0:16], (1, 1, 4))
    c4[:, :, 64:128] = np.tile(m["csq"][:, :, 16:32], (1, 1, 4))
    m["csq4"] = c4
    mq = np.zeros((9, NWIN), np.float32)
    mq[0, :] = 0.0 if s == 1 else NEGM
    qo = np.arange(NOWN)
    lq = (qo % 512) // 64
    for j in range(8):
        mq[1 + j, HALO + qo[lq < j]] = NEGM
    m["maskq"] = mq.astype(ml_dtypes.bfloat16)
    return m


_NC_CACHE = {}


def kernel(**inputs):
    debug = inputs.pop("_debug", None)
    stop = inputs.pop("_stop", 99)
    key = (tuple(sorted(debug.items())) if debug else None, stop)
    if key not in _NC_CACHE:
        _NC_CACHE[key] = build(debug, stop)
    nc = _NC_CACHE[key]
    sh = _prep_shared(inputs)
    in_maps = []
    for c in range(8):
        m = dict(sh)
        m.update(_prep_core(inputs, c // 2, c % 2))
        in_maps.append(m)
    res = run_bass_kernel_spmd(nc, in_maps, core_ids=list(range(8)))
    out = np.zeros((4, 4096, D), np.float32)
    for c in range(8):
        out[c // 2, (c % 2) * 2048:(c % 2 + 1) * 2048] = np.asarray(res.results[c]["out"], np.float32)
    if debug:
        kernel.last_debug = [{k: np.asarray(v) for k, v in r.items() if k.startswith("dbg_")} for r in res.results]
    return out
```
